# Optimizing a Trainium2 kernel written in Bass

```python
import jax, jax.numpy as jnp
from jax import lax
import numpy as np

D_MODEL = 2048
BATCH = 4
SEQ = 4096
DEPTH = 4

N_MIXERS = 2
GRID_W = 64
N_META = 16
NORM_EPS = 1e-6

ATTN_HEADS = 16
ATTN_KV_HEADS = 8
ATTN_GROUP = ATTN_HEADS // ATTN_KV_HEADS
ATTN_HEAD_DIM = 128
ATTN_WIDTH = ATTN_HEADS * ATTN_HEAD_DIM
ATTN_KV_WIDTH = ATTN_KV_HEADS * ATTN_HEAD_DIM
ATTN_IN = 2 * ATTN_WIDTH + 2 * ATTN_KV_WIDTH
Q_BLOCK = 128
ROPE_THETA = 10000.0
ROPE_AXIS_DIM = ATTN_HEAD_DIM // 2

GLA_HEADS = 4
GLA_KEY_DIM = D_MODEL // 2
GLA_VALUE_DIM = D_MODEL
GLA_HEAD_K = GLA_KEY_DIM // GLA_HEADS
GLA_HEAD_V = GLA_VALUE_DIM // GLA_HEADS
GLA_GATE_RANK = 16
GLA_GATE_NORMALIZER = 16.0
GLA_CHUNK = 64
GLA_IN = 2 * GLA_KEY_DIM + 2 * GLA_VALUE_DIM + 2 * GLA_GATE_RANK

N_ATTN_LAYERS = (DEPTH + 1) // 2
N_GLA_LAYERS = DEPTH // 2

kernel_name = "hybrid_attn_gla_interleaved_encoder"


def rmsnorm(x, w):
    xf = x.astype(jnp.float32)
    y = xf * lax.rsqrt(jnp.mean(xf * xf, axis=-1, keepdims=True) + NORM_EPS)
    return (y * w.astype(jnp.float32)).astype(x.dtype)


def axial_rope_angles(seq_len):
    rows = seq_len // GRID_W
    row = jnp.repeat(jnp.arange(rows), GRID_W).astype(jnp.float32)
    col = jnp.tile(jnp.arange(GRID_W), rows).astype(jnp.float32)
    inv_freq = ROPE_THETA ** (-jnp.arange(0, ROPE_AXIS_DIM, 2, dtype=jnp.float32) / ROPE_AXIS_DIM)
    meta = jnp.zeros((N_META, ROPE_AXIS_DIM // 2), jnp.float32)
    ang_row = jnp.concatenate([meta, row[:, None] * inv_freq[None]], axis=0)
    ang_col = jnp.concatenate([meta, col[:, None] * inv_freq[None]], axis=0)
    return ang_row, ang_col


def _rotate(x, ang):
    cos = jnp.cos(ang)[None, :, None, :].astype(x.dtype)
    sin = jnp.sin(ang)[None, :, None, :].astype(x.dtype)
    x1, x2 = jnp.split(x, 2, axis=-1)
    return jnp.concatenate([x1 * cos - x2 * sin, x2 * cos + x1 * sin], axis=-1)


def apply_axial_rope(x, ang_row, ang_col):
    return jnp.concatenate([_rotate(x[..., :ROPE_AXIS_DIM], ang_row),
                            _rotate(x[..., ROPE_AXIS_DIM:], ang_col)], axis=-1)


def _attend(qb, k, v):
    s = jnp.einsum('bqkgd,bskd->bkgqs', qb, k) * (ATTN_HEAD_DIM ** -0.5)
    p = jax.nn.softmax(s.astype(jnp.float32), axis=-1).astype(v.dtype)
    return jnp.einsum('bkgqs,bskd->bqkgd', p, v)


def attention_mixer(h, w_in, q_norm, k_norm, w_out, ang_row, ang_col):
    B, L, _ = h.shape
    proj = h @ w_in
    q, k, v, gate = jnp.split(
        proj, [ATTN_WIDTH, ATTN_WIDTH + ATTN_KV_WIDTH, ATTN_WIDTH + 2 * ATTN_KV_WIDTH], axis=-1)
    q = q.reshape(B, L, ATTN_HEADS, ATTN_HEAD_DIM)
    k = k.reshape(B, L, ATTN_KV_HEADS, ATTN_HEAD_DIM)
    v = v.reshape(B, L, ATTN_KV_HEADS, ATTN_HEAD_DIM)
    q = apply_axial_rope(rmsnorm(q, q_norm), ang_row, ang_col)
    k = apply_axial_rope(rmsnorm(k, k_norm), ang_row, ang_col)
    q = q.reshape(B, L, ATTN_KV_HEADS, ATTN_GROUP, ATTN_HEAD_DIM)
    o_meta = _attend(q[:, :N_META], k, v)
    n_real = L - N_META
    nb = n_real // Q_BLOCK
    q_blocks = q[:, N_META:].reshape(B, nb, Q_BLOCK, ATTN_KV_HEADS, ATTN_GROUP, ATTN_HEAD_DIM)
    o_real = lax.map(lambda qb: _attend(qb, k, v), jnp.swapaxes(q_blocks, 0, 1))
    o_real = jnp.swapaxes(o_real, 0, 1).reshape(B, n_real, ATTN_KV_HEADS, ATTN_GROUP, ATTN_HEAD_DIM)
    o = jnp.concatenate([o_meta, o_real], axis=1).reshape(B, L, ATTN_WIDTH)
    return (o * jax.nn.silu(gate)) @ w_out


def gla_chunked(q, k, v, g, strict):
    B, H, T, dk = q.shape
    dv = v.shape[-1]
    n = T // GLA_CHUNK

    def chunks(t):
        return t.reshape(B, H, n, GLA_CHUNK, t.shape[-1]).astype(jnp.float32)

    qc, kc, vc, gc = chunks(q), chunks(k), chunks(v), chunks(g)
    b = jnp.cumsum(gc, axis=3)
    b_last = b[:, :, :, -1:, :]
    q_dec = qc * jnp.exp(b)
    k_inv = kc * jnp.exp(-b)
    k_end = kc * jnp.exp(b_last - b)
    scores = jnp.einsum('bhncd,bhnsd->bhncs', q_dec, k_inv)
    mask = jnp.tril(jnp.ones((GLA_CHUNK, GLA_CHUNK), bool), k=-1 if strict else 0)
    o_intra = jnp.einsum('bhncs,bhnse->bhnce', jnp.where(mask, scores, 0.0), vc)

    def step(S, xs):
        q_t, k_t, v_t, dl = xs
        o = jnp.einsum('bhcd,bhde->bhce', q_t, S)
        S = S * dl[..., None] + jnp.einsum('bhcd,bhce->bhde', k_t, v_t)
        return S, o

    xs = (jnp.moveaxis(q_dec, 2, 0), jnp.moveaxis(k_end, 2, 0), jnp.moveaxis(vc, 2, 0),
          jnp.moveaxis(jnp.exp(b_last[:, :, :, 0, :]), 2, 0))
    S0 = jnp.zeros((B, H, dk, dv), jnp.float32)
    _, o_inter = lax.scan(step, S0, xs)
    o = o_intra + jnp.moveaxis(o_inter, 0, 2)
    return o.reshape(B, H, T, dv).astype(q.dtype)


def gla_mixer(h, w_in, gk_up, gk_bias, o_norm, w_out):
    B, L, _ = h.shape
    pad = GLA_CHUNK - N_META
    hp = jnp.pad(h, ((0, 0), (pad, 0), (0, 0)))
    T = L + pad
    proj = hp @ w_in
    s1 = GLA_KEY_DIM
    s2 = s1 + GLA_KEY_DIM
    s3 = s2 + GLA_VALUE_DIM
    s4 = s3 + GLA_VALUE_DIM
    s5 = s4 + GLA_GATE_RANK
    q, k, v, g_out, lr_f, lr_b = jnp.split(proj, [s1, s2, s3, s4, s5], axis=-1)

    def heads(t, d):
        return t.reshape(B, T, GLA_HEADS, d).transpose(0, 2, 1, 3)

    def log_gate(lr, up, bias):
        z = (lr @ up + bias).astype(jnp.float32)
        return heads(jax.nn.log_sigmoid(z) / GLA_GATE_NORMALIZER, GLA_HEAD_K)

    q = heads(q, GLA_HEAD_K) * (GLA_HEAD_K ** -0.5)
    k = heads(k, GLA_HEAD_K)
    v = heads(v, GLA_HEAD_V)
    flip = lambda t: jnp.flip(t, axis=2)
    o_f = gla_chunked(q, k, v, log_gate(lr_f, gk_up[0], gk_bias[0]), strict=False)
    o_b = flip(gla_chunked(flip(q), flip(k), flip(v),
                           flip(log_gate(lr_b, gk_up[1], gk_bias[1])), strict=True))
    o = (o_f + o_b).transpose(0, 2, 1, 3)[:, pad:]
    o = rmsnorm(o, o_norm).reshape(B, L, GLA_VALUE_DIM)
    return (o * jax.nn.silu(g_out[:, pad:])) @ w_out


def setup_inputs(seed: int = 0) -> dict:
    key = jax.random.key(seed)
    ks = jax.random.split(key, 13)
    f32 = jnp.float32
    nrm = lambda k, shape, scale: jax.random.normal(k, shape, f32) * scale
    return {
        "x": nrm(ks[0], (BATCH, SEQ, D_MODEL), 1.0),
        "meta_tokens": nrm(ks[1], (N_META, D_MODEL), 1.0),
        "pre_norm": 1.0 + nrm(ks[2], (DEPTH, D_MODEL), 0.02),
        "post_norm": 1.0 + nrm(ks[3], (DEPTH, D_MODEL), 0.02),
        "attn_w_in": nrm(ks[4], (N_ATTN_LAYERS, D_MODEL, ATTN_IN), D_MODEL ** -0.5),
        "attn_q_norm": 1.0 + nrm(ks[5], (N_ATTN_LAYERS, ATTN_HEAD_DIM), 0.02),
        "attn_k_norm": 1.0 + nrm(ks[6], (N_ATTN_LAYERS, ATTN_HEAD_DIM), 0.02),
        "attn_w_out": nrm(ks[7], (N_ATTN_LAYERS, ATTN_WIDTH, D_MODEL), ATTN_WIDTH ** -0.5),
        "gla_w_in": nrm(ks[8], (N_GLA_LAYERS, D_MODEL, GLA_IN), D_MODEL ** -0.5),
        "gla_gk_up": nrm(ks[9], (N_GLA_LAYERS, 2, GLA_GATE_RANK, GLA_KEY_DIM), GLA_GATE_RANK ** -0.5),
        "gla_gk_bias": nrm(ks[10], (N_GLA_LAYERS, 2, GLA_KEY_DIM), 0.1),
        "gla_o_norm": 1.0 + nrm(ks[11], (N_GLA_LAYERS, GLA_HEAD_V), 0.02),
        "gla_w_out": nrm(ks[12], (N_GLA_LAYERS, GLA_VALUE_DIM, D_MODEL), GLA_VALUE_DIM ** -0.5),
    }


def reference(x, meta_tokens, pre_norm, post_norm, attn_w_in, attn_q_norm, attn_k_norm,
              attn_w_out, gla_w_in, gla_gk_up, gla_gk_bias, gla_o_norm, gla_w_out):
    B, S, D = x.shape
    meta = jnp.broadcast_to(meta_tokens.astype(x.dtype)[None], (B, N_META, D))
    h = jnp.concatenate([meta, x], axis=1)
    ang_row, ang_col = axial_rope_angles(S)
    for i in range(DEPTH):
        y = rmsnorm(h, pre_norm[i])
        j = i // N_MIXERS
        if i % N_MIXERS == 0:
            y = attention_mixer(y, attn_w_in[j], attn_q_norm[j], attn_k_norm[j], attn_w_out[j],
                                ang_row, ang_col)
        else:
            y = gla_mixer(y, gla_w_in[j], gla_gk_up[j], gla_gk_bias[j], gla_o_norm[j], gla_w_out[j])
        h = h + rmsnorm(y, post_norm[i])
    return h[:, N_META:]
```

```python
import numpy as np
import ml_dtypes
from contextlib import ExitStack
import concourse.bass as bass
import concourse.mybir as mybir
from concourse.bass_utils import run_bass_kernel_spmd

F32 = mybir.dt.float32
BF16 = mybir.dt.bfloat16
AF = mybir.ActivationFunctionType
ALU = mybir.AluOpType

D = 2048
KT = 16
NMETA = 16
EPS = 1e-6
N_CORES = 4


class Buf:
    __slots__ = ("ap", "writers", "readers")

    def __init__(self, ap=None):
        self.ap = ap
        self.writers = []
        self.readers = []


class Ins:
    __slots__ = ("eng", "fn", "deps", "idx", "needed", "dma", "semkey", "semval", "cnt", "inc")

    def __init__(self, eng, fn, dma=False):
        self.eng = eng
        self.fn = fn
        self.deps = []
        self.idx = -1
        self.needed = False
        self.dma = dma
        self.semkey = None
        self.semval = 0
        self.cnt = 0
        self.inc = 16


class Sched:
    ENGS = ("pe", "act", "dve", "pool", "sp")
    NDMA = {"sp": 24, "pool": 12, "act": 8, "spw": 8}

    def __init__(self, nc):
        self.nc = nc
        self.prog = {e: [] for e in self.ENGS}
        self.dma_rr = {q: 0 for q in self.NDMA}
        self.dma_cnt = {}
        self.dma_last = {}
        self.pending_barrier = {e: None for e in self.ENGS}

    def _track(self, ins, reads, writes):
        deps = {}
        for b in reads:
            for w in b.writers:
                deps[id(w)] = w
        for b in writes:
            for w in b.writers:
                deps[id(w)] = w
            for r in b.readers:
                deps[id(r)] = r
        deps.pop(id(ins), None)
        pb = self.pending_barrier[ins.eng]
        if pb is not None:
            for d in pb:
                deps[id(d)] = d
            self.pending_barrier[ins.eng] = None
        ins.deps = list(deps.values())
        for b in reads:
            b.readers.append(ins)
        for b in writes:
            b.writers = [ins]
            b.readers = []

    def op(self, eng, fn, reads=(), writes=()):
        ins = Ins(eng, fn)
        self._track(ins, reads, writes)
        ins.idx = len(self.prog[eng])
        self.prog[eng].append(ins)
        return ins

    def dma(self, q, out_ap, in_ap, reads=(), writes=(), sempool=None, **kw):
        ins = Ins(q, lambda e: e.dma_start(out=out_ap, in_=in_ap, **kw), dma=True)
        self._track(ins, reads, writes)
        sp_ = sempool or q
        j = self.dma_rr[sp_]
        self.dma_rr[sp_] = (j + 1) % self.NDMA[sp_]
        key = ("dma", sp_, j)
        prev = self.dma_last.get(key)
        if prev is not None:
            ins.deps.append(prev)
        self.dma_cnt[key] = self.dma_cnt.get(key, 0) + 16
        ins.semkey = key
        ins.semval = self.dma_cnt[key]
        self.dma_last[key] = ins
        ins.idx = len(self.prog[q])
        self.prog[q].append(ins)
        return ins

    def cc(self, fn, reads=(), writes=(), sempool="cc"):
        ins = Ins("pool", fn, dma=True)
        ins.inc = 1
        self._track(ins, reads, writes)
        key = ("dma", sempool, 0)
        prev = self.dma_last.get(key)
        if prev is not None:
            ins.deps.append(prev)
        self.dma_cnt[key] = self.dma_cnt.get(key, 0) + 1
        ins.semkey = key
        ins.semval = self.dma_cnt[key]
        self.dma_last[key] = ins
        ins.idx = len(self.prog["pool"])
        self.prog["pool"].append(ins)
        return ins

    def barrier(self):
        deps = []
        for e in self.ENGS:
            if self.prog[e]:
                last = None
                for ins in reversed(self.prog[e]):
                    if not ins.dma:
                        last = ins
                        break
                if last is not None:
                    deps.append(last)
        deps.extend(v for k, v in self.dma_last.items() if k[1] not in ("spw", "ccw"))
        for e in self.ENGS:
            self.pending_barrier[e] = list(deps)

    def emit(self):
        nc = self.nc
        fin = Ins("sp", None)
        fin.deps = list(self.dma_last.values())
        fin.idx = len(self.prog["sp"])
        self.prog["sp"].append(fin)
        for e in self.ENGS:
            seen = {}
            for ins in self.prog[e]:
                best = {}
                for d in ins.deps:
                    if d.dma:
                        key, v = d.semkey, d.semval
                    else:
                        if d.eng == e and e == "pe":
                            continue
                        key, v = d.eng, d.idx + 1
                    if key not in best or best[key][0] < v:
                        best[key] = (v, d)
                kept = []
                for key, (v, d) in best.items():
                    if seen.get(key, 0) >= v:
                        continue
                    seen[key] = v
                    kept.append(d)
                    d.needed = True
                ins.deps = kept
        for e in self.ENGS:
            c = 0
            for ins in self.prog[e]:
                if ins.needed and not ins.dma:
                    c += 1
                    ins.cnt = c
        with ExitStack() as st:
            sems = {}
            for e in ("pe", "act", "dve", "pool", "sp"):
                sems[e] = st.enter_context(nc.semaphore("s_" + e))
            for key in self.dma_cnt:
                sems[key] = st.enter_context(nc.semaphore("d_%s_%d" % (key[1], key[2])))
            block = st.enter_context(nc.Block())
            prog = self.prog

            def run(ename):
                def body(eng):
                    for ins in prog[ename]:
                        waits = {}
                        for d in ins.deps:
                            if d.dma:
                                key, v = d.semkey, d.semval
                            else:
                                key, v = d.eng, d.cnt
                            if waits.get(key, 0) < v:
                                waits[key] = v
                        for key, v in waits.items():
                            eng.wait_ge(sems[key], v)
                        if ins.fn is None:
                            continue
                        r = ins.fn(eng)
                        if ins.dma:
                            r.then_inc(sems[ins.semkey], ins.inc)
                        elif ins.needed:
                            r.then_inc(sems[ename], 1)
                return body

            block.tensor(run("pe"))
            block.scalar(run("act"))
            block.vector(run("dve"))
            block.gpsimd(run("pool"))
            block.sync(run("sp"))


class Rot:
    def __init__(self, bufs):
        self.bufs = bufs
        self.i = 0

    def next(self):
        b = self.bufs[self.i]
        self.i = (self.i + 1) % len(self.bufs)
        return b


class Builder:
    def __init__(self, NR, layers=(0, 1, 2, 3), debug=False, groups=((0, 1),)):
        self.groups = [list(g) for g in groups]
        self.NR = NR
        self.P = NR + NMETA
        self.layers = layers
        self.debug = debug
        P = self.P
        self.tiles = [(i * 128, 128) for i in range(NR // 128)] + [(NR, NMETA)]
        self.NT = len(self.tiles)
        self.chunks = [(c * 512, 512, list(range(4 * c, 4 * c + 4))) for c in range(NR // 512)]
        self.chunks.append((NR, NMETA, [self.NT - 1]))
        self.gchunks = [(NR, NMETA)] + [(i * 64, 64) for i in range(NR // 64)]
        self.NCH = len(self.gchunks)
        nc = bass.Bass("TRN2", target_bir_lowering=False)
        self.nc = nc
        self.S = Sched(nc)
        self.dbufs = {}
        I = lambda name, shape, dt: nc.dram_tensor(name, shape, dt, kind="ExternalInput").ap()
        self.h0 = I("h0", [P, D], F32)
        self.prenT = I("prenT", [4, 128, KT], F32)
        self.postn = I("postn", [4, D], F32)
        self.wfa = {}
        for nm, ncol in (("a_win", 6144), ("a_wout", D), ("g_win", 6144), ("g_wout", D)):
            t_ = I(nm, [2, D, ncol], F32)
            for j in range(2):
                self.wfa[nm, j] = t_[j]
        self.g_wlr = I("g_wlr", [2, D, 32], F32)
        self.wq = []
        self.wq_b = 0
        self.wq_g = 0
        self.a_qn = I("a_qn", [128, 2], F32)
        self.a_kn = I("a_kn", [128, 2], F32)
        self.g_up = I("g_up", [2, 2, 16, 1024], F32)
        self.g_bias = I("g_bias", [2, 2, 1024], F32)
        self.g_on = I("g_on", [2, 512], F32)
        self.cosT = I("cosT", [128, P], F32)
        self.sinT = I("sinT", [128, P], F32)
        self.c_rperm = I("c_rperm", [128, 128], F32)
        self.c_ident = I("c_ident", [128, 128], BF16)
        self.c_tri = I("c_tri", [2, 2, 128, 64], F32)
        self.c_mask = I("c_mask", [2, 64, 256], F32)
        self.selw = I("selw", [128, 2], F32)
        self.out = nc.dram_tensor("out", [NR, D], F32, kind="ExternalOutput").ap()
        kind = "ExternalOutput" if debug else "Internal"
        Sc = lambda name, shape, dt: nc.dram_tensor(name, shape, dt, kind=kind).ap()
        self.h_scr = Sc("h_scr", [P, D], F32)
        self.yT_scr = Sc("yT_scr", [self.NT, 128, D], BF16)
        self.qT_scr = Sc("qT_scr", [16, 128, P], BF16)
        self.kT_loc = [nc.dram_tensor("kT_loc%d" % i, [128, P], BF16) for i in range(8)]
        self.va_loc = [nc.dram_tensor("va_loc%d" % i, [P, 128], BF16) for i in range(8)]
        self.kT_all = [nc.dram_tensor("kT_all%d" % i, [2 * 128, P], BF16) for i in range(8)]
        self.va_all = [nc.dram_tensor("va_all%d" % i, [2 * P, 128], BF16) for i in range(8)]
        self.S_loc = nc.dram_tensor("S_loc", [128, 4096], F32)
        self.S_all = nc.dram_tensor("S_all", [256, 4096], F32)
        self.sgT_scr = Sc("sgT_scr", [16, 128, P], F32)
        self.ogTa_scr = Sc("ogTa_scr", [16, 128, P], BF16)
        self.q32_scr = Sc("q32_scr", [8, 128, P], F32)
        self.k32_scr = Sc("k32_scr", [8, 128, P], F32)
        self.vg_scr = Sc("vg_scr", [P, D], BF16)
        self.sg_scr = Sc("sg_scr", [P, D], F32)
        self.of_scr = Sc("of_scr", [P, D], F32)
        self.ogTg_scr = Sc("ogTg_scr", [self.NCH, 128, 1024], BF16)

    def db(self, *key):
        b = self.dbufs.get(key)
        if b is None:
            b = Buf()
            self.dbufs[key] = b
        return b

    def uname(self, name):
        self.uid = getattr(self, "uid", 0) + 1
        return "%s_u%d" % (name, self.uid)

    def sb(self, st, name, shape, dt, n=1):
        bufs = [Buf(st.enter_context(self.nc.sbuf_tensor(self.uname(name), shape, dt))[:]) for i in range(n)]
        return bufs[0] if n == 1 else Rot(bufs)

    def ps(self, st, name, shape, dt, n=1):
        bufs = [Buf(st.enter_context(self.nc.psum_tensor(self.uname(name), shape, dt))[:]) for i in range(n)]
        return bufs[0] if n == 1 else Rot(bufs)

    def dump(self, name, buf, ap, shape, dt):
        if not self.debug:
            return
        t = self.nc.dram_tensor("dbg_" + name, shape, dt, kind="ExternalOutput").ap()
        self.S.dma("pool", t, ap, reads=[buf])

    def w_bounce(self):
        if self.wq_b >= len(self.wq):
            return
        nm, j = self.wq[self.wq_b]
        self.wq_b += 1
        for r in range(8):
            self.S.dma("sp", self.wb_[nm, j][r * 128:(r + 1) * 128, :], self.wh[nm][j, r * 128:(r + 1) * 128, :],
                       writes=[self.db("wb", nm, j, r)], sempool="spw")

    def w_gather(self):
        if self.wq_g >= len(self.wq):
            return
        nm, j = self.wq[self.wq_g]
        self.wq_g += 1
        src, dst = self.wb_[nm, j], self.wf[nm, j]
        self.S.cc(lambda e: e.collective_compute("AllGather", ALU.bypass, replica_groups=self.groups,
                                                 ins=[src.ap().opt()], outs=[dst.ap().opt()]),
                  reads=[self.db("wb", nm, j, r) for r in range(8)], writes=[self.db("wf", nm, j)], sempool="ccw")

    def w_step(self):
        self.w_gather()
        self.w_bounce()

    def load_consts(self, st):
        S = self.S
        self.ident = self.sb(st, "ident", [128, 128], BF16)
        self.rperm = self.sb(st, "rperm", [128, 128], F32)
        self.ones32 = self.sb(st, "ones32", [128, 128], F32)
        self.onesbf = self.sb(st, "onesbf", [128, 128], BF16)
        self.wpreT = self.sb(st, "wpreT", [128, 4, KT], F32)
        self.qn = self.sb(st, "qn", [128, 2], F32)
        self.kn = self.sb(st, "kn", [128, 2], F32)
        S.dma("sp", self.ident.ap, self.c_ident, writes=[self.ident])
        S.dma("sp", self.rperm.ap, self.c_rperm, writes=[self.rperm])
        S.dma("sp", self.wpreT.ap, self.prenT.rearrange("l p k -> p l k"), writes=[self.wpreT])
        S.dma("sp", self.qn.ap, self.a_qn, writes=[self.qn])
        S.dma("sp", self.kn.ap, self.a_kn, writes=[self.kn])
        S.op("dve", lambda e: e.memset(self.ones32.ap, 1.0), writes=[self.ones32])
        S.op("dve", lambda e: e.memset(self.onesbf.ap, 1.0), writes=[self.onesbf])

    def p1_alloc(self, st):
        self.p1_junk = self.sb(st, "p1junk", [128, D], BF16)
        self.p1_ss = self.sb(st, "p1ss", [128, 1], F32, 2)
        self.p1_yn = self.sb(st, "p1yn", [128, D], BF16, 2)
        self.p1_yT = self.sb(st, "p1yT", [128, KT, 128], BF16, 2)
        self.p1_tp = self.ps(st, "p1tp", [128, 8, 128], BF16, 2)

    def p1_tile(self, hb, ti, li):
        S = self.S
        r0, n = self.tiles[ti]
        ss = self.p1_ss.next()
        yn = self.p1_yn.next()
        yT = self.p1_yT.next()
        junk = self.p1_junk
        S.op("act", lambda e: e.activation(out=junk.ap[:n], in_=hb.ap[:n], func=AF.Square, accum_out=ss.ap[:n]),
             reads=[hb], writes=[junk, ss])
        S.op("act", lambda e: e.activation(out=ss.ap[:n], in_=ss.ap[:n], func=AF.Sqrt, scale=1.0 / D, bias=EPS),
             reads=[ss], writes=[ss])
        S.op("dve", lambda e: e.reciprocal(out=ss.ap[:n], in_=ss.ap[:n]), reads=[ss], writes=[ss])
        S.op("dve", lambda e: e.tensor_scalar(out=yn.ap[:n], in0=hb.ap[:n], scalar1=ss.ap[:n, 0:1], scalar2=None,
                                              op0=ALU.mult), reads=[hb, ss], writes=[yn])
        yield
        for g in range(2):
            tp = self.p1_tp.next()
            for k in range(8):
                kt = g * 8 + k
                S.op("pe", lambda e, k=k, kt=kt, tp=tp: e.transpose(
                    out=tp.ap[:, k, :n], in_=yn.ap[:n, kt * 128:(kt + 1) * 128], identity=self.ident.ap[:n, :n]),
                    reads=[yn, self.ident], writes=[tp])
            S.op("dve", lambda e, g=g, tp=tp: e.tensor_tensor(
                out=yT.ap[:, g * 8:(g + 1) * 8, :n], in0=tp.ap[:, :, :n],
                in1=self.wpreT.ap[:, li, g * 8:(g + 1) * 8].unsqueeze(2).broadcast_to([128, 8, n]), op=ALU.mult),
                reads=[tp, self.wpreT], writes=[yT])
        dst = self.yT_scr[ti].rearrange("p (k t) -> p k t", k=KT)[:, :, :n]
        S.dma("pool", dst, yT.ap[:, :, :n], reads=[yT], writes=[self.db("yT", ti)])

    def p0(self):
        S = self.S
        with ExitStack() as st:
            self.p1_alloc(st)
            hb = self.sb(st, "p0h", [128, D], F32, 3)
            for ti, (r0, n) in enumerate(self.tiles):
                h = hb.next()
                S.dma("sp", h.ap[:n], self.h0[r0:r0 + n, :], writes=[h])
                for _ in self.p1_tile(h, ti, self.layers[0]):
                    pass
            S.barrier()

    def p2(self, li):
        S = self.S
        is_attn = (li % 2 == 0)
        j = li // 2
        wname = "a_win" if is_attn else "g_win"
        wdep = self.db("wf", wname, j)
        Wv = self.wfa[wname, j].rearrange("(k p) n -> p k n", p=128)
        Wlr = self.g_wlr[j].rearrange("(k p) n -> p k n", p=128)
        self.w_step()
        if is_attn:
            jobs = [("aq", h, h * 128, 128) for h in range(16)]
            jobs += [("ak", h, 2048 + h * 128, 128) for h in range(8)]
            jobs += [("av", c, 3072 + c * 256, 256) for c in range(4)]
            jobs += [("ag", f, 4096 + f * 128, 128) for f in range(16)]
        else:
            jobs = [("lr", d, d * 16, 16) for d in range(2)]
            jobs += [("gq", d, d * 128, 128) for d in range(8)]
            jobs += [("gk", d, 1024 + d * 128, 128) for d in range(8)]
            jobs += [("gv", c, 2048 + c * 256, 256) for c in range(8)]
            jobs += [("gg", c, 4096 + c * 256, 256) for c in range(8)]
        nch = len(self.chunks)
        half = nch if self.NT <= 17 else max(1, nch // 2)
        sblocks = [self.chunks[:half], self.chunks[half:]]
        sblocks = [s for s in sblocks if s]
        maxt = max(sum(len(c[2]) for c in s) for s in sblocks)
        maxr = max(sum(c[1] for c in s) for s in sblocks)
        with ExitStack() as st:
            yT = st.enter_context(self.nc.sbuf_tensor(self.uname("p2yT"), [128, maxt, KT, 128], BF16))
            ytb = [Buf(yT[:, i]) for i in range(maxt)]
            wst = self.sb(st, "p2wst", [128, KT, 256], F32, 2)
            wbf = self.sb(st, "p2wbf", [128, KT, 256], BF16, 3)
            pp = self.ps(st, "p2pp", [128, 512], F32, 3)
            if is_attn:
                cos = self.sb(st, "p2cos", [128, maxr], F32)
                sin = self.sb(st, "p2sin", [128, maxr], F32)
                sq = self.sb(st, "p2sq", [128, 512], F32, 2)
                rs = self.sb(st, "p2rs", [128, 512], F32, 2)
                qn = self.sb(st, "p2qn", [128, 512], F32, 2)
                t1 = self.sb(st, "p2t1", [128, 512], F32, 2)
                t2 = self.sb(st, "p2t2", [128, 512], F32, 2)
                qr = self.sb(st, "p2qr", [128, 512], BF16, 3)
                ssp = self.ps(st, "p2ssp", [128, 512], F32, 2)
                rot = self.ps(st, "p2rot", [128, 512], F32, 2)
            o32 = self.sb(st, "p2o32", [128, 512], F32, 3)
            obf = self.sb(st, "p2obf", [128, 256], BF16, 3)
            jobn = 0
            pending = []

            def advance():
                for g_ in list(pending):
                    try:
                        next(g_)
                    except StopIteration:
                        pending.remove(g_)

            def qk_epi(kind, idx, p, n, r0, lo):
                wn = self.qn if kind == "aq" else self.kn
                a_sq, a_rs, a_qn, a_t1, a_t2, a_qr = sq.next(), rs.next(), qn.next(), t1.next(), t2.next(), qr.next()
                a_ssp, a_rot = ssp.next(), rot.next()
                S.op("act", lambda e: e.activation(out=a_sq.ap[:, :n], in_=p.ap[:, :n], func=AF.Square),
                     reads=[p], writes=[a_sq])
                yield
                S.op("pe", lambda e: e.matmul(out=a_ssp.ap[:, :n], lhsT=self.ones32.ap, rhs=a_sq.ap[:, :n],
                                              start=True, stop=True), reads=[a_sq, self.ones32], writes=[a_ssp])
                S.op("act", lambda e: e.activation(out=a_rs.ap[:, :n], in_=a_ssp.ap[:, :n], func=AF.Sqrt,
                                                   scale=1.0 / 128, bias=EPS), reads=[a_ssp], writes=[a_rs])
                S.op("dve", lambda e: e.reciprocal(out=a_rs.ap[:, :n], in_=a_rs.ap[:, :n]), reads=[a_rs], writes=[a_rs])
                S.op("dve", lambda e: e.scalar_tensor_tensor(
                    out=a_qn.ap[:, :n], in0=p.ap[:, :n], scalar=wn.ap[:, j:j + 1], in1=a_rs.ap[:, :n],
                    op0=ALU.mult, op1=ALU.mult), reads=[p, a_rs, wn], writes=[a_qn])
                yield
                S.op("pe", lambda e: e.matmul(out=a_rot.ap[:, :n], lhsT=self.rperm.ap, rhs=a_qn.ap[:, :n],
                                              start=True, stop=True), reads=[a_qn, self.rperm], writes=[a_rot])
                S.op("pool", lambda e: e.tensor_tensor(out=a_t1.ap[:, :n], in0=a_qn.ap[:, :n], in1=cos.ap[:, lo:lo + n],
                                                       op=ALU.mult), reads=[a_qn, cos], writes=[a_t1])
                S.op("dve", lambda e: e.tensor_tensor(out=a_t2.ap[:, :n], in0=a_rot.ap[:, :n], in1=sin.ap[:, lo:lo + n],
                                                      op=ALU.mult), reads=[a_rot, sin], writes=[a_t2])
                S.op("pool", lambda e: e.tensor_tensor(out=a_qr.ap[:, :n], in0=a_t1.ap[:, :n], in1=a_t2.ap[:, :n],
                                                       op=ALU.add), reads=[a_t1, a_t2], writes=[a_qr])
                if kind == "aq":
                    dst_ap = self.qT_scr[idx, :, r0:r0 + n]
                else:
                    dst_ap = self.kT_loc[idx][:, r0:r0 + n]
                S.dma("pool", dst_ap, a_qr.ap[:, :n], reads=[a_qr], writes=[self.db(kind, idx, r0)])

            for sbk in sblocks:
                tmap = {}
                rbase = sbk[0][0]
                for (r0, n, tl) in sbk:
                    for ti in tl:
                        slot = len(tmap)
                        tmap[ti] = slot
                        tn = self.tiles[ti][1]
                        src = self.yT_scr[ti].rearrange("p (k t) -> p k t", k=KT)[:, :, :tn]
                        S.dma("sp", ytb[slot].ap[:, :, :tn], src, reads=[self.db("yT", ti)], writes=[ytb[slot]])
                if is_attn:
                    tot = sum(c[1] for c in sbk)
                    S.dma("sp", cos.ap[:, :tot], self.cosT[:, rbase:rbase + tot], writes=[cos])
                    S.dma("sp", sin.ap[:, :tot], self.sinT[:, rbase:rbase + tot], writes=[sin])
                def load_w(job):
                    (kind_, idx_, c0_, ncol_) = job
                    ws = wst.next()
                    wb_ = wbf.next()
                    if kind_ == "lr":
                        S.dma("sp", ws.ap[:, :, :ncol_], Wlr[:, :, c0_:c0_ + ncol_], writes=[ws])
                    else:
                        S.dma("sp", ws.ap[:, :, :ncol_], Wv[:, :, c0_:c0_ + ncol_], reads=[wdep], writes=[ws])
                    S.op("act", lambda e: e.activation(out=wb_.ap[:, :, :ncol_], in_=ws.ap[:, :, :ncol_], func=AF.Copy),
                         reads=[ws], writes=[wb_])
                    return wb_

                wnext = load_w(jobs[0])
                for ji, (kind, idx, c0, ncol) in enumerate(jobs):
                    wb = wnext
                    if ji + 1 < len(jobs):
                        wnext = load_w(jobs[ji + 1])
                    if kind in ("av", "gv", "gg"):
                        for (r0, n, tl) in sbk:
                            for ti in tl:
                                tr0, tn = self.tiles[ti]
                                yb = ytb[tmap[ti]]
                                p = pp.next()
                                for kt in range(KT):
                                    S.op("pe", lambda e, p=p, yb=yb, wb=wb, kt=kt, tn=tn, ncol=ncol: e.matmul(
                                        out=p.ap[:tn, :ncol], lhsT=yb.ap[:, kt, :tn], rhs=wb.ap[:, kt, :ncol],
                                        start=(kt == 0), stop=(kt == KT - 1)), reads=[yb, wb], writes=[p])
                                advance()
                                if kind == "gg":
                                    o = o32.next()
                                    S.op("act", lambda e, o=o, p=p, tn=tn, ncol=ncol: e.activation(
                                        out=o.ap[:tn, :ncol], in_=p.ap[:tn, :ncol], func=AF.Silu), reads=[p], writes=[o])
                                    S.dma("pool", self.sg_scr[tr0:tr0 + tn, idx * 256:(idx + 1) * 256], o.ap[:tn, :ncol],
                                          reads=[o], writes=[self.db("sg", ti, idx)])
                                else:
                                    o = obf.next()
                                    S.op("dve", lambda e, o=o, p=p, tn=tn, ncol=ncol: e.tensor_copy(
                                        out=o.ap[:tn, :ncol], in_=p.ap[:tn, :ncol]), reads=[p], writes=[o])
                                    if kind == "av":
                                        for u_ in range(2):
                                            S.dma("pool", self.va_loc[idx * 2 + u_][tr0:tr0 + tn, :],
                                                  o.ap[:tn, u_ * 128:(u_ + 1) * 128], reads=[o],
                                                  writes=[self.db("av", ti, idx * 2 + u_)])
                                    else:
                                        S.dma("pool", self.vg_scr[tr0:tr0 + tn, idx * 256:(idx + 1) * 256], o.ap[:tn, :ncol],
                                              reads=[o], writes=[self.db(kind, ti, idx)])
                        continue
                    M = ncol
                    for ci, (r0, n, tl) in enumerate(sbk):
                        p = pp.next()
                        s0 = tmap[tl[0]]
                        for kt in range(KT):
                            if len(tl) == 1:
                                rhs_fn = lambda kt=kt, s0=s0, n=n: yT[:, s0, kt, :n]
                            else:
                                rhs_fn = lambda kt=kt, s0=s0, tl=tl: yT[:, s0:s0 + len(tl), kt, :]
                            S.op("pe", lambda e, p=p, wb=wb, kt=kt, rhs_fn=rhs_fn, n=n, M=M: e.matmul(
                                out=p.ap[:M, :n], lhsT=wb.ap[:, kt, :M], rhs=rhs_fn(),
                                start=(kt == 0), stop=(kt == KT - 1)),
                                reads=[ytb[tmap[t]] for t in tl] + [wb], writes=[p])
                        lo = r0 - rbase
                        if kind in ("aq", "ak"):
                            pending.append(qk_epi(kind, idx, p, n, r0, lo))
                        advance()
                        if kind == "ag":
                            o = o32.next()
                            S.op("act", lambda e, o=o, p=p, n=n: e.activation(out=o.ap[:, :n], in_=p.ap[:, :n], func=AF.Silu),
                                 reads=[p], writes=[o])
                            S.dma("pool", self.sgT_scr[idx, :, r0:r0 + n], o.ap[:, :n], reads=[o],
                                  writes=[self.db("ag", idx, r0)])
                        elif kind in ("gq", "gk"):
                            o = o32.next()
                            S.op("dve", lambda e, o=o, p=p, n=n: e.tensor_copy(out=o.ap[:, :n], in_=p.ap[:, :n]),
                                 reads=[p], writes=[o])
                            dst = self.q32_scr if kind == "gq" else self.k32_scr
                            S.dma("pool", dst[idx, :, r0:r0 + n], o.ap[:, :n], reads=[o],
                                  writes=[self.db(kind, idx, r0)])
                        elif kind == "lr":
                            lb = self.lrT[idx]
                            S.op("dve", lambda e, lb=lb, p=p, n=n, r0=r0: e.tensor_copy(
                                out=lb.ap[0:16, r0:r0 + n], in_=p.ap[:16, :n]), reads=[p], writes=[lb])
                while pending:
                    advance()
            if is_attn:
                for kv in range(8):
                    S.cc(lambda e, kv=kv: e.collective_compute(
                        "AllGather", ALU.bypass, replica_groups=self.groups,
                        ins=[self.kT_loc[kv].ap().opt()], outs=[self.kT_all[kv].ap().opt()]),
                        reads=[self.db("ak", kv, c[0]) for c in self.chunks], writes=[self.db("kTall", kv)])
                    S.cc(lambda e, kv=kv: e.collective_compute(
                        "AllGather", ALU.bypass, replica_groups=self.groups,
                        ins=[self.va_loc[kv].ap().opt()], outs=[self.va_all[kv].ap().opt()]),
                        reads=[self.db("av", ti, kv) for ti in range(self.NT)], writes=[self.db("vaall", kv)])
            S.barrier()

    def p3_attn(self, li):
        S = self.S
        self.w_step()
        P = self.P
        NT = self.NT
        scale = 128.0 ** -0.5
        with ExitStack() as st:
            nfull = NT - 1
            NKT = 2 * nfull + 1
            ktiles = [(i * 128, 128, i) for i in range(nfull)] + [(self.NR, NMETA, nfull)]
            ktiles += [(P + i * 128, 128, nfull + 1 + i) for i in range(nfull)]
            KTb = self.sb(st, "p3kt", [128, 2 * P], BF16, 2)
            Vb = self.sb(st, "p3v", [128, NKT, 128], BF16, 2)
            qb = self.sb(st, "p3q", [128, 512], BF16, 2)
            sgb = self.sb(st, "p3sg", [128, 512], F32, 2)
            pT = self.sb(st, "p3pT", [128, 512], BF16, 3)
            rec = self.sb(st, "p3rec", [128, 512], F32, 2)
            tt = self.sb(st, "p3tt", [128, 512], F32, 2)
            og = self.sb(st, "p3og", [128, 512], BF16, 2)
            sps = self.ps(st, "p3s", [128, 512], F32, 3)
            ops = self.ps(st, "p3o", [128, 512], F32, 2)
            sums = self.ps(st, "p3sum", [128, 512], F32, 2)
            accb = self.sb(st, "p3acc", [128, 512], F32, 2)
            LA = 2
            groups = []
            for kv in range(8):
                for g in range(2):
                    for ch in self.chunks:
                        groups.append((kv, g, ch))
            units = [(gi, ti) for gi in range(len(groups)) for ti in range(NKT)]
            gstate = {}
            kvstate = {}

            def begin(gi):
                kv, g, (r0, n, tl) = groups[gi]
                if kv not in kvstate:
                    K = KTb.next()
                    V = Vb.next()
                    for r in range(2):
                        i_ = S.dma("sp", K.ap[:, r * P:(r + 1) * P],
                                   self.kT_all[kv][r * 128:(r + 1) * 128, :],
                                   reads=[self.db("kTall", kv)], writes=[K] if r == 0 else [])
                        if r == 1:
                            K.writers.append(i_)
                    vdeps = [self.db("vaall", kv)]
                    S.dma("sp", V.ap[:, 0:nfull, :],
                          self.va_all[kv][0:nfull * 128, :].rearrange("(t p) d -> p t d", p=128),
                          reads=vdeps, writes=[V])
                    i_ = S.dma("sp", V.ap[:NMETA, nfull, :], self.va_all[kv][self.NR:self.NR + NMETA, :],
                               reads=vdeps, writes=[])
                    V.writers.append(i_)
                    i_ = S.dma("sp", V.ap[:, nfull + 1:NKT, :],
                               self.va_all[kv][P:P + nfull * 128, :].rearrange("(t p) d -> p t d", p=128),
                               reads=vdeps, writes=[])
                    V.writers.append(i_)
                    kvstate[kv] = (K, V)
                K, V = kvstate[kv]
                h = kv * 2 + g
                q = qb.next()
                sg = sgb.next()
                S.dma("sp", q.ap[:, :n], self.qT_scr[h, :, r0:r0 + n], reads=[self.db("aq", h, r0)], writes=[q])
                S.dma("sp", sg.ap[:, :n], self.sgT_scr[h, :, r0:r0 + n], reads=[self.db("ag", h, r0)], writes=[sg])
                gstate[gi] = dict(K=K, V=V, q=q, sg=sg, o=ops.next(), s=sums.next(), acc=accb.next(), h=h, r0=r0, n=n, pt={})

            def front(gi, ti):
                st_ = gstate[gi]
                K, q, n = st_["K"], st_["q"], st_["n"]
                k0, nk, vs = ktiles[ti]
                sp_ = sps.next()
                p_ = pT.next()
                st_["pt"][ti] = p_
                S.op("pe", lambda e: e.matmul(out=sp_.ap[:nk, :n], lhsT=K.ap[:, k0:k0 + nk], rhs=q.ap[:, :n],
                                              start=True, stop=True), reads=[K, q], writes=[sp_])
                S.op("act", lambda e: e.activation(out=p_.ap[:nk, :n], in_=sp_.ap[:nk, :n], func=AF.Exp, scale=scale),
                     reads=[sp_], writes=[p_])

            def back(gi, ti):
                st_ = gstate[gi]
                V, n, o_ps, s_ps = st_["V"], st_["n"], st_["o"], st_["s"]
                k0, nk, vs = ktiles[ti]
                p_ = st_["pt"].pop(ti)
                S.op("pe", lambda e: e.matmul(out=o_ps.ap[:, :n], lhsT=V.ap[:nk, vs, :], rhs=p_.ap[:nk, :n],
                                              start=(ti == 0), stop=(ti == NKT - 1)), reads=[V, p_], writes=[o_ps])
                acc = st_["acc"]
                if ti == 0:
                    S.op("dve", lambda e: e.tensor_copy(out=acc.ap[:, :n], in_=p_.ap[:, :n]), reads=[p_], writes=[acc])
                else:
                    S.op("dve", lambda e: e.tensor_tensor(out=acc.ap[:nk, :n], in0=acc.ap[:nk, :n], in1=p_.ap[:nk, :n],
                                                          op=ALU.add), reads=[acc, p_], writes=[acc])
                if ti == NKT - 1:
                    S.op("pe", lambda e: e.matmul(out=s_ps.ap[:, :n], lhsT=self.ones32.ap, rhs=acc.ap[:, :n],
                                                  start=True, stop=True), reads=[self.ones32, acc], writes=[s_ps])
                    sg, h, r0 = st_["sg"], st_["h"], st_["r0"]
                    r_, t_, og_ = rec.next(), tt.next(), og.next()
                    S.op("dve", lambda e: e.reciprocal(out=r_.ap[:, :n], in_=s_ps.ap[:, :n]), reads=[s_ps], writes=[r_])
                    S.op("dve", lambda e: e.tensor_tensor(out=t_.ap[:, :n], in0=o_ps.ap[:, :n], in1=r_.ap[:, :n], op=ALU.mult),
                         reads=[o_ps, r_], writes=[t_])
                    S.op("pool", lambda e: e.tensor_tensor(out=og_.ap[:, :n], in0=t_.ap[:, :n], in1=sg.ap[:, :n], op=ALU.mult),
                         reads=[t_, sg], writes=[og_])
                    S.dma("pool", self.ogTa_scr[h, :, r0:r0 + n], og_.ap[:, :n], reads=[og_],
                          writes=[self.db("ogTa", h, r0)])
                    del gstate[gi]

            for ui in range(len(units) + LA):
                if ui < len(units):
                    gi, ti = units[ui]
                    if ti == 0:
                        begin(gi)
                    front(gi, ti)
                if ui >= LA:
                    gi, ti = units[ui - LA]
                    back(gi, ti)
            S.barrier()

    def p4(self, li, pos):
        S = self.S
        is_attn = (li % 2 == 0)
        j = li // 2
        first = (pos == 0)
        last = (pos == len(self.layers) - 1)
        woname = "a_wout" if is_attn else "g_wout"
        wodep = self.db("wf", woname, j)
        Wo = self.wfa[woname, j].rearrange("(k p) n -> p k n", p=128)
        self.w_step()
        with ExitStack() as st:
            if not last:
                self.p1_alloc(st)
            wo = st.enter_context(self.nc.sbuf_tensor(self.uname("p4wo"), [128, KT, D], BF16))
            wob = [Buf(wo[:, 2 * i:2 * i + 2, :]) for i in range(8)]
            wst = self.sb(st, "p4wst", [128, 2, D], F32, 2)
            wpost = self.sb(st, "p4wpost", [128, D], F32)
            ogt = self.sb(st, "p4og", [128, KT, 128], BF16, 2)
            hb = self.sb(st, "p4h", [128, D], F32, 2)
            tb = self.sb(st, "p4t", [128, D], F32, 2)
            hn = self.sb(st, "p4hn", [128, D], F32, 2)
            ss4 = self.sb(st, "p4ss4", [128, 4], F32, 2)
            ss = self.sb(st, "p4ss", [128, 1], F32, 2)
            junk = self.sb(st, "p4junk", [128, 512], BF16)
            yps = self.ps(st, "p4y", [128, 512], F32, 6 if not last else 8)
            S.dma("sp", wpost.ap, self.postn[li:li + 1, :].broadcast_to([128, D]), writes=[wpost])
            for i in range(8):
                w = wst.next()
                S.dma("sp", w.ap, Wo[:, 2 * i:2 * i + 2, :], reads=[wodep], writes=[w])
                if i % 2 == 0:
                    S.op("act", lambda e, w=w, i=i: e.activation(out=wob[i].ap, in_=w.ap, func=AF.Copy),
                         reads=[w], writes=[wob[i]])
                else:
                    S.op("pool", lambda e, w=w, i=i: e.tensor_copy(out=wob[i].ap, in_=w.ap), reads=[w], writes=[wob[i]])
            prev_p1 = None
            for ti, (r0, n) in enumerate(self.tiles):
                og = ogt.next()
                if is_attn:
                    deps = [self.db("ogTa", h, c[0]) for h in range(16) for c in self.chunks if c[0] <= r0 < c[0] + c[1]]
                    S.dma("sp", og.ap[:, :, :n], self.ogTa_scr[:, :, r0:r0 + n].rearrange("f p t -> p f t"),
                          reads=deps, writes=[og])
                    lfn = lambda ft, og=og, n=n: og.ap[:, ft, :n]
                elif n == 128:
                    ci = 1 + r0 // 64
                    S.dma("sp", og.ap[:, :, 0:64], self.ogTg_scr[ci].rearrange("p (f t) -> p f t", f=KT),
                          reads=[self.db("ogTg", ci)], writes=[og])
                    i2 = S.dma("sp", og.ap[:, :, 64:128], self.ogTg_scr[ci + 1].rearrange("p (f t) -> p f t", f=KT),
                               reads=[self.db("ogTg", ci + 1)], writes=[])
                    og.writers.append(i2)
                    lfn = lambda ft, og=og: og.ap[:, ft, :]
                else:
                    S.dma("sp", og.ap[:, :, :n], self.ogTg_scr[0].rearrange("p (f t) -> p f t", f=KT)[:, :, :n],
                          reads=[self.db("ogTg", 0)], writes=[og])
                    lfn = lambda ft, og=og, n=n: og.ap[:, ft, :n]
                h = hb.next()
                hsrc = self.h0 if first else self.h_scr
                S.dma("sp", h.ap[:n], hsrc[r0:r0 + n, :], reads=[] if first else [self.db("h", ti)], writes=[h])
                ys = []
                s4 = ss4.next()
                for nb in range(4):
                    y = yps.next()
                    ys.append(y)
                    for ft in range(KT):
                        S.op("pe", lambda e, y=y, ft=ft, nb=nb, lfn=lfn, n=n: e.matmul(
                            out=y.ap[:n, :], lhsT=lfn(ft), rhs=wo[:, ft, nb * 512:(nb + 1) * 512],
                            start=(ft == 0), stop=(ft == KT - 1)), reads=[og, wob[ft // 2]], writes=[y])
                    S.op("act", lambda e, y=y, nb=nb, s4=s4, n=n: e.activation(
                        out=junk.ap[:n], in_=y.ap[:n], func=AF.Square, accum_out=s4.ap[:n, nb:nb + 1]),
                        reads=[y], writes=[junk, s4] if nb == 0 else [junk], )
                    if nb > 0:
                        s4.writers.append(S.prog["act"][-1])
                if prev_p1 is not None:
                    next(prev_p1, None)
                    prev_p1 = None
                s1 = ss.next()
                S.op("dve", lambda e, s1=s1, s4=s4, n=n: e.tensor_reduce(
                    out=s1.ap[:n], in_=s4.ap[:n], axis=mybir.AxisListType.X, op=ALU.add), reads=[s4], writes=[s1])
                S.op("act", lambda e, s1=s1, n=n: e.activation(out=s1.ap[:n], in_=s1.ap[:n], func=AF.Sqrt, scale=1.0 / D, bias=EPS),
                     reads=[s1], writes=[s1])
                S.op("dve", lambda e, s1=s1, n=n: e.reciprocal(out=s1.ap[:n], in_=s1.ap[:n]), reads=[s1], writes=[s1])
                t = tb.next()
                tparts = []
                for nb in range(4):
                    S.op("dve", lambda e, t=t, y=ys[nb], s1=s1, nb=nb, n=n: e.scalar_tensor_tensor(
                        out=t.ap[:n, nb * 512:(nb + 1) * 512], in0=y.ap[:n], scalar=s1.ap[:n, 0:1],
                        in1=wpost.ap[:n, nb * 512:(nb + 1) * 512], op0=ALU.mult, op1=ALU.mult),
                        reads=[ys[nb], s1, wpost], writes=[t] if nb == 0 else [])
                    if nb > 0:
                        t.writers.append(S.prog["dve"][-1])
                hnew = hn.next()
                S.op("pool", lambda e, hnew=hnew, t=t, h=h, n=n: e.tensor_tensor(
                    out=hnew.ap[:n], in0=t.ap[:n], in1=h.ap[:n], op=ALU.add), reads=[t, h], writes=[hnew])
                if last:
                    if r0 < self.NR:
                        S.dma("pool", self.out[r0:r0 + n, :], hnew.ap[:n], reads=[hnew], writes=[self.db("out", ti)])
                else:
                    S.dma("pool", self.h_scr[r0:r0 + n, :], hnew.ap[:n], reads=[hnew], writes=[self.db("h", ti)])
                    prev_p1 = self.p1_tile(hnew, ti, self.layers[pos + 1])
                    next(prev_p1)
            if prev_p1 is not None:
                next(prev_p1, None)
            S.barrier()

    def p3_gla(self, li):
        S = self.S
        self.w_step()
        j = li // 2
        NCH = self.NCH
        with ExitStack() as st:
            up = [self.sb(st, "gup%d" % d, [128, 1024], F32) for d in range(2)]
            tri = [[self.sb(st, "gtri%d_%d" % (d, k), [128, 64], F32) for k in range(2)] for d in range(2)]
            mask = [self.sb(st, "gmask%d" % d, [64, 4, 64], F32) for d in range(2)]
            wo_bc = self.sb(st, "gwobc", [128, 512], F32)
            for d in range(2):
                S.op("pool", lambda e, d=d: e.memset(up[d].ap, 0.0), writes=[up[d]])
                S.dma("sp", up[d].ap[0:16, :], self.g_up[j, d], writes=[up[d]])
                S.dma("sp", up[d].ap[32:33, :], self.g_bias[j, d:d + 1, :], writes=[up[d]])
                for k in range(2):
                    S.dma("sp", tri[d][k].ap, self.c_tri[d, k], writes=[tri[d][k]])
                S.dma("sp", mask[d].ap, self.c_mask[d].rearrange("s (h c) -> s h c", h=4), writes=[mask[d]])
            S.dma("sp", wo_bc.ap, self.g_on[j:j + 1, :].broadcast_to([128, 512]), writes=[wo_bc])
            Sst = st.enter_context(self.nc.sbuf_tensor(self.uname("gS"), [128, 8, 512], F32))
            Sbf_t = st.enter_context(self.nc.sbuf_tensor(self.uname("gSbf"), [128, 8, 512], BF16))
            Sb = [Buf(Sst[:, i, :]) for i in range(8)]
            Sbf = [Buf(Sbf_t[:, i, :]) for i in range(8)]
            q32 = self.sb(st, "gq32", [128, 8, 64], F32, 2)
            k32 = self.sb(st, "gk32", [128, 8, 64], F32, 2)
            vb = self.sb(st, "gv", [64, D], BF16, 4)
            ofb = self.sb(st, "gof", [64, D], F32, 2)
            sgb = self.sb(st, "gsg", [64, D], F32, 2)
            eb = self.sb(st, "ge", [64, 1024], F32, 2)
            gpb = self.sb(st, "ggp", [128, 1024], F32, 2)
            for b_ in gpb.bufs:
                S.op("pool", lambda e, b_=b_: e.memset(b_.ap, 0.0), writes=[b_])
            E1b = self.sb(st, "gE1", [128, 8, 64], F32, 3)
            E2b = self.sb(st, "gE2", [128, 8, 64], F32, 2)
            qdb = self.sb(st, "gqd", [128, 8, 64], BF16, 3)
            ke32b = self.sb(st, "gke32", [128, 8, 64], F32, 2)
            kib = self.sb(st, "gki", [128, 8, 64], BF16, 2)
            keTb = self.sb(st, "gkeT", [128, 8, 64], BF16, 2)
            kendb = self.sb(st, "gkend", [64, 8, 128], BF16, 2)
            pTb = self.sb(st, "gpT", [64, 4, 64], BF16, 2)
            osb = self.sb(st, "gos", [64, D], F32, 2)
            ss4b = self.sb(st, "gss4", [64, 4], F32, 2)
            ogb = self.sb(st, "gog", [64, D], BF16, 2)
            ogTb = self.sb(st, "gogT", [128, KT, 64], BF16, 2)
            junk = self.sb(st, "gjunk", [64, 512], BF16)
            zb = self.ps(st, "gz", [128, 512], F32, 2)
            scp = self.ps(st, "gsc", [64, 8, 64], F32)
            tpx = self.ps(st, "gtpx", [128, 1024], BF16)
            tpk = tpx.ap.rearrange("p (a b) -> p a b", a=8)
            tpo = tpx.ap.rearrange("p (a b) -> p a b", a=KT)
            ops = self.ps(st, "go", [64, 512], F32, 2)
            upd = self.ps(st, "gupd", [128, 512], F32, 2)
            xt = self.sb(st, "gxt", [128, 512], F32, 4)
            selw = self.sb(st, "gselw", [128, 2], F32)
            S.dma("sp", selw.ap, self.selw, writes=[selw])
            for dr in range(2):
                order = list(range(NCH)) if dr == 0 else list(range(NCH - 1, -1, -1))
                if dr == 1:
                    S.dma("pool", self.S_loc[:, :], Sst[:, :, :].rearrange("p d e -> p (d e)"), reads=Sb,
                          writes=[self.db("Sloc")])
                    S.cc(lambda e: e.collective_compute("AllGather", ALU.bypass, replica_groups=self.groups,
                                                        ins=[self.S_loc.ap().opt()], outs=[self.S_all.ap().opt()]),
                         reads=[self.db("Sloc")], writes=[self.db("Sall")])
                    for dt in range(8):
                        x0, x1 = xt.next(), xt.next()
                        S.dma("sp", x0.ap, self.S_all[0:128, dt * 512:(dt + 1) * 512], reads=[self.db("Sall")], writes=[x0])
                        S.dma("sp", x1.ap, self.S_all[128:256, dt * 512:(dt + 1) * 512], reads=[self.db("Sall")], writes=[x1])
                        S.op("dve", lambda e, dt=dt, x0=x0: e.tensor_scalar(
                            out=Sb[dt].ap, in0=x0.ap, scalar1=selw.ap[:, 0:1], scalar2=None, op0=ALU.mult),
                            reads=[x0, selw], writes=[Sb[dt]])
                        S.op("dve", lambda e, dt=dt, x1=x1: e.scalar_tensor_tensor(
                            out=Sb[dt].ap, in0=x1.ap, scalar=selw.ap[:, 1:2], in1=Sb[dt].ap, op0=ALU.mult, op1=ALU.add),
                            reads=[x1, selw, Sb[dt]], writes=[Sb[dt]])
                        S.op("act", lambda e, dt=dt: e.activation(out=Sbf[dt].ap, in_=Sb[dt].ap, func=AF.Copy),
                             reads=[Sb[dt]], writes=[Sbf[dt]])
                def chunk_gen(step, ci, dr=dr):
                    r0, n = self.gchunks[ci]
                    firstc = (dr == 0 and step == 0)
                    lastc = (dr == 1 and step == NCH - 1)
                    up_d, mk, lr = up[dr], mask[dr], self.lrT[dr]
                    trk = tri[dr][0 if n == 64 else 1]
                    lastcol = (n - 1) if dr == 0 else 0
                    cidx = [c[0] for c in self.chunks if c[0] <= r0 < c[0] + c[1]][0]
                    tidx = [t for t, (a, b) in enumerate(self.tiles) if a <= r0 < a + b][0]
                    q3, k3, v = q32.next(), k32.next(), vb.next()
                    S.dma("sp", q3.ap[:, :, :n], self.q32_scr[:, :, r0:r0 + n].rearrange("d p t -> p d t"),
                          reads=[self.db("gq", d, cidx) for d in range(8)], writes=[q3])
                    S.dma("sp", k3.ap[:, :, :n], self.k32_scr[:, :, r0:r0 + n].rearrange("d p t -> p d t"),
                          reads=[self.db("gk", d, cidx) for d in range(8)], writes=[k3])
                    S.dma("sp", v.ap[:n], self.vg_scr[r0:r0 + n, :], reads=[self.db("gv", tidx, c) for c in range(8)], writes=[v])
                    e_, gp = eb.next(), gpb.next()
                    for hf in range(2):
                        z = zb.next()
                        S.op("pe", lambda e, z=z, hf=hf: e.matmul(
                            out=z.ap[:n, :], lhsT=lr.ap[:, r0:r0 + n], rhs=up_d.ap[:, hf * 512:(hf + 1) * 512],
                            start=True, stop=True), reads=[lr, up_d], writes=[z])
                        S.op("act", lambda e, z=z, hf=hf: e.activation(
                            out=e_.ap[:n, hf * 512:(hf + 1) * 512], in_=z.ap[:n, :], func=AF.Exp, scale=-1.0),
                            reads=[z], writes=[e_] if hf == 0 else [])
                        if hf == 1:
                            e_.writers.append(S.prog["act"][-1])
                    S.op("act", lambda e: e.activation(out=gp.ap[:n], in_=e_.ap[:n], func=AF.Ln, bias=1.0),
                         reads=[e_], writes=[gp])
                    yield
                    bT = zb.next()
                    bv = bT.ap.rearrange("p (d c) -> p d c", d=8)
                    for dt in range(8):
                        S.op("pe", lambda e, dt=dt: e.matmul(
                            out=bv[:, dt, :n], lhsT=gp.ap[:, dt * 128:(dt + 1) * 128], rhs=trk.ap[:, :n],
                            start=True, stop=True), reads=[gp, trk], writes=[bT])
                    E1, E2 = E1b.next(), E2b.next()
                    S.op("act", lambda e: e.activation(out=E1.ap[:, :, :n], in_=bv[:, :, :n], func=AF.Exp),
                         reads=[bT], writes=[E1])
                    S.op("act", lambda e: e.activation(out=E2.ap[:, :, :n], in_=bv[:, :, :n], func=AF.Exp, scale=-1.0),
                         reads=[bT], writes=[E2])
                    qd, ke32, ki, keT = qdb.next(), ke32b.next(), kib.next(), keTb.next()
                    S.op("dve", lambda e: e.scalar_tensor_tensor(
                        out=qd.ap[:, :, :n], in0=q3.ap[:, :, :n], scalar=0.0625, in1=E1.ap[:, :, :n],
                        op0=ALU.mult, op1=ALU.mult), reads=[q3, E1], writes=[qd])
                    S.op("dve", lambda e: e.tensor_tensor(
                        out=ke32.ap[:, :, :n], in0=k3.ap[:, :, :n], in1=E2.ap[:, :, :n], op=ALU.mult),
                        reads=[k3, E2], writes=[ke32])
                    S.op("act", lambda e: e.activation(out=ki.ap[:, :, :n], in_=ke32.ap[:, :, :n], func=AF.Copy),
                         reads=[ke32], writes=[ki])
                    if not lastc:
                        S.op("dve", lambda e: e.tensor_tensor(
                            out=keT.ap[:, :, :n], in0=ke32.ap[:, :, :n],
                            in1=E1.ap[:, :, lastcol:lastcol + 1].broadcast_to([128, 8, n]), op=ALU.mult),
                            reads=[ke32, E1], writes=[keT])
                    yield
                    if dr == 1:
                        of, sg = ofb.next(), sgb.next()
                        S.dma("sp", of.ap[:n], self.of_scr[r0:r0 + n, :], reads=[self.db("of", ci)], writes=[of])
                        S.dma("sp", sg.ap[:n], self.sg_scr[r0:r0 + n, :], reads=[self.db("sg", tidx, c) for c in range(8)],
                              writes=[sg])
                        S.op("pool", lambda e: e.tensor_tensor(
                            out=sg.ap[:n].rearrange("p (h e) -> p h e", h=4), in0=sg.ap[:n].rearrange("p (h e) -> p h e", h=4),
                            in1=wo_bc.ap[:n].unsqueeze(1).broadcast_to([n, 4, 512]), op=ALU.mult),
                            reads=[sg, wo_bc], writes=[sg])
                    if not lastc:
                        kend = kendb.next()
                        for dt in range(8):
                            S.op("pe", lambda e, dt=dt: e.transpose(
                                out=tpk[:n, dt, :], in_=keT.ap[:, dt, :n], identity=self.ident.ap),
                                reads=[keT, self.ident], writes=[tpx])
                        S.op("act", lambda e: e.activation(out=kend.ap[:n], in_=tpk[:n], func=AF.Copy),
                             reads=[tpx], writes=[kend])
                    for hh in range(4):
                        for u in range(2):
                            dt = hh * 2 + u
                            S.op("pe", lambda e, hh=hh, dt=dt, u=u: e.matmul(
                                out=scp.ap[:n, hh, :n], lhsT=ki.ap[:, dt, :n], rhs=qd.ap[:, dt, :n],
                                start=(u == 0), stop=(u == 1)), reads=[ki, qd], writes=[scp])
                    pT = pTb.next()
                    S.op("dve", lambda e: e.tensor_tensor(
                        out=pT.ap[:n, :, :n], in0=scp.ap[:n, 0:4, :n], in1=mk.ap[:n, :, :n], op=ALU.mult),
                        reads=[scp, mk], writes=[pT])
                    yield
                    if dr == 1:
                        osum, s4, ogx = osb.next(), ss4b.next(), ogb.next()
                    else:
                        osum = osb.next()
                    for hh in range(4):
                        o = ops.next()
                        S.op("pe", lambda e, o=o, hh=hh: e.matmul(
                            out=o.ap[:n, :], lhsT=pT.ap[:n, hh, :n], rhs=v.ap[:n, hh * 512:(hh + 1) * 512],
                            start=True, stop=firstc), reads=[pT, v], writes=[o])
                        if not firstc:
                            for u in range(2):
                                dt = hh * 2 + u
                                S.op("pe", lambda e, o=o, dt=dt, u=u: e.matmul(
                                    out=o.ap[:n, :], lhsT=qd.ap[:, dt, :n], rhs=Sbf[dt].ap,
                                    start=False, stop=(u == 1)), reads=[qd, Sbf[dt]], writes=[o])
                        hs = slice(hh * 512, (hh + 1) * 512)
                        if dr == 0:
                            S.op("act", lambda e, o=o, hs=hs: e.activation(
                                out=osum.ap[:n, hs], in_=o.ap[:n, :], func=AF.Copy),
                                reads=[o], writes=[osum] if hh == 0 else [])
                            if hh > 0:
                                osum.writers.append(S.prog["act"][-1])
                        else:
                            S.op("dve", lambda e, o=o, hs=hs: e.tensor_tensor(
                                out=osum.ap[:n, hs], in0=o.ap[:n, :], in1=of.ap[:n, hs], op=ALU.add),
                                reads=[o, of], writes=[osum] if hh == 0 else [])
                            if hh > 0:
                                osum.writers.append(S.prog["dve"][-1])
                            S.op("act", lambda e, hs=hs, hh=hh: e.activation(
                                out=junk.ap[:n], in_=osum.ap[:n, hs], func=AF.Square, accum_out=s4.ap[:n, hh:hh + 1]),
                                reads=[osum], writes=[junk, s4] if hh == 0 else [junk])
                            if hh > 0:
                                s4.writers.append(S.prog["act"][-1])
                        if not lastc:
                            for u in range(2):
                                dt = hh * 2 + u
                                ub = upd.next()
                                S.op("pe", lambda e, dt=dt, hh=hh, ub=ub: e.matmul(
                                    out=ub.ap, lhsT=kend.ap[:n, dt, :], rhs=v.ap[:n, hh * 512:(hh + 1) * 512],
                                    start=True, stop=True), reads=[kend, v], writes=[ub])
                                if firstc:
                                    S.op("dve", lambda e, dt=dt, ub=ub: e.tensor_copy(out=Sb[dt].ap, in_=ub.ap),
                                         reads=[ub], writes=[Sb[dt]])
                                else:
                                    S.op("dve", lambda e, dt=dt, ub=ub: e.scalar_tensor_tensor(
                                        out=Sb[dt].ap, in0=Sb[dt].ap, scalar=E1.ap[:, dt, lastcol:lastcol + 1], in1=ub.ap,
                                        op0=ALU.mult, op1=ALU.add), reads=[Sb[dt], E1, ub], writes=[Sb[dt]])
                                if dt % 2 == 0:
                                    S.op("act", lambda e, dt=dt: e.activation(out=Sbf[dt].ap, in_=Sb[dt].ap, func=AF.Copy),
                                         reads=[Sb[dt]], writes=[Sbf[dt]])
                                else:
                                    S.op("dve", lambda e, dt=dt: e.tensor_copy(out=Sbf[dt].ap, in_=Sb[dt].ap),
                                         reads=[Sb[dt]], writes=[Sbf[dt]])
                    if dr == 0:
                        S.dma("pool", self.of_scr[r0:r0 + n, :], osum.ap[:n], reads=[osum], writes=[self.db("of", ci)])
                    else:
                        S.op("act", lambda e: e.activation(out=s4.ap[:n], in_=s4.ap[:n], func=AF.Sqrt,
                                                           scale=1.0 / 512, bias=EPS), reads=[s4], writes=[s4])
                        S.op("dve", lambda e: e.reciprocal(out=s4.ap[:n], in_=s4.ap[:n]), reads=[s4], writes=[s4])
                        for hh in range(4):
                            hs = slice(hh * 512, (hh + 1) * 512)
                            S.op("dve", lambda e, hs=hs, hh=hh: e.scalar_tensor_tensor(
                                out=ogx.ap[:n, hs], in0=osum.ap[:n, hs], scalar=s4.ap[:n, hh:hh + 1], in1=sg.ap[:n, hs],
                                op0=ALU.mult, op1=ALU.mult), reads=[osum, s4, sg], writes=[ogx] if hh == 0 else [])
                            if hh > 0:
                                ogx.writers.append(S.prog["dve"][-1])
                        for ft in range(KT):
                            S.op("pe", lambda e, ft=ft: e.transpose(
                                out=tpo[:, ft, :n], in_=ogx.ap[:n, ft * 128:(ft + 1) * 128], identity=self.ident.ap[:n, :n]),
                                reads=[ogx, self.ident], writes=[tpx])
                        ogT = ogTb.next()
                        S.op("act", lambda e: e.activation(out=ogT.ap[:, :, :n], in_=tpo[:, :, :n], func=AF.Copy),
                             reads=[tpx], writes=[ogT])
                        S.dma("pool", self.ogTg_scr[ci].rearrange("p (f t) -> p f t", f=KT)[:, :, :n], ogT.ap[:, :, :n],
                              reads=[ogT], writes=[self.db("ogTg", ci)])

                gens = []

                def tick():
                    for g_ in list(gens):
                        try:
                            next(g_)
                        except StopIteration:
                            gens.remove(g_)

                for step, ci in enumerate(order):
                    gens.append(chunk_gen(step, ci))
                    tick()
                while gens:
                    tick()
            S.barrier()

    def build(self):
        with ExitStack() as st:
            self.load_consts(st)
            self.w_bounce()
            self.w_gather()
            self.w_bounce()
            self.p0()
            for pos, li in enumerate(self.layers):
                if li % 2 == 0:
                    self.p2(li)
                    self.p3_attn(li)
                else:
                    with ExitStack() as st2:
                        self.lrT = [self.sb(st2, "lrT%d_%d" % (li, d), [128, self.P], F32) for d in range(2)]
                        for lb in self.lrT:
                            self.S.op("pool", lambda e, lb=lb: e.memset(lb.ap, 0.0), writes=[lb])
                            self.S.op("pool", lambda e, lb=lb: e.memset(lb.ap[32:33, :], 1.0), writes=[lb])
                        self.p2(li)
                        self.p3_gla(li)
                    self.S.barrier()
                self.p4(li, pos)
            self.S.emit()
        return self.nc


def make_consts(NR):
    P = NR + NMETA
    t = np.arange(NR)
    row = (t // 64).astype(np.float32)
    col = (t % 64).astype(np.float32)
    inv = (np.float32(10000.0) ** (-np.arange(0, 64, 2, dtype=np.float32) / np.float32(64))).astype(np.float32)
    ang = np.zeros((P, 128), np.float32)
    ar = row[:, None] * inv[None]
    ac = col[:, None] * inv[None]
    ang[:NR, 0:32] = ar
    ang[:NR, 32:64] = ar
    ang[:NR, 64:96] = ac
    ang[:NR, 96:128] = ac
    cos = np.cos(ang).astype(np.float32)
    sin = np.sin(ang).astype(np.float32)
    sgn = np.ones(128, np.float32)
    sgn[0:32] = -1
    sgn[64:96] = -1
    sin = sin * sgn[None]
    rperm = np.zeros((128, 128), np.float32)
    for m in range(128):
        src = m + 32 if (m % 64) < 32 else m - 32
        rperm[src, m] = 1
    jj = np.arange(64)
    tri = np.zeros((2, 2, 128, 64), np.float32)
    tri[0, 0, :64] = np.where(jj[:, None] <= jj[None, :], -1.0 / 16, 0.0)
    tri[1, 0, :64] = np.where(jj[:, None] >= jj[None, :], -1.0 / 16, 0.0)
    tri[:, 1, :16, :16] = tri[:, 0, :16, :16]
    mask = np.zeros((2, 64, 4, 64), np.float32)
    mask[0] = np.where(jj[:, None] <= jj[None, :], 1.0, 0.0)[:, None, :]
    mask[1] = np.where(jj[:, None] > jj[None, :], 1.0, 0.0)[:, None, :]
    return dict(cosT=np.ascontiguousarray(cos.T), sinT=np.ascontiguousarray(sin.T), c_rperm=rperm,
                c_ident=np.eye(128).astype(ml_dtypes.bfloat16), c_tri=tri,
                c_mask=np.ascontiguousarray(mask.reshape(2, 64, 256)))


def make_in_maps_pairs(inputs, NRL):
    f = lambda a: np.ascontiguousarray(np.asarray(a, dtype=np.float32))
    x = f(inputs["x"])
    meta = f(inputs["meta_tokens"])
    B, SEQ, _ = x.shape
    assert SEQ == 2 * NRL
    base = dict(
        prenT=np.ascontiguousarray(f(inputs["pre_norm"]).reshape(4, KT, 128).transpose(0, 2, 1)),
        postn=f(inputs["post_norm"]),
        a_qn=np.ascontiguousarray(f(inputs["attn_q_norm"]).T), a_kn=np.ascontiguousarray(f(inputs["attn_k_norm"]).T),
        g_on=f(inputs["gla_o_norm"]),
    )
    gw = f(inputs["gla_w_in"])
    base.update(a_win=f(inputs["attn_w_in"]), a_wout=f(inputs["attn_w_out"]),
                g_win=np.ascontiguousarray(gw[:, :, :6144]), g_wout=f(inputs["gla_w_out"]))
    wlr = np.ascontiguousarray(gw[:, :, 6144:6176])
    wlr_sw = np.ascontiguousarray(np.concatenate([gw[:, :, 6160:6176], gw[:, :, 6144:6160]], axis=2))
    gup = f(inputs["gla_gk_up"])
    gb = f(inputs["gla_gk_bias"])
    cst = make_consts(2 * NRL)
    cosG, sinG = cst["cosT"], cst["sinT"]
    jj = np.arange(64)
    def masks(m0, m1):
        m = np.zeros((2, 64, 4, 64), np.float32)
        m[0] = m0[:, None, :]
        m[1] = m1[:, None, :]
        return np.ascontiguousarray(m.reshape(2, 64, 256))
    le = (jj[:, None] <= jj[None, :]).astype(np.float32)
    lt = (jj[:, None] < jj[None, :]).astype(np.float32)
    gt = (jj[:, None] > jj[None, :]).astype(np.float32)
    ge = (jj[:, None] >= jj[None, :]).astype(np.float32)
    shared = {k: cst[k] for k in ("c_rperm", "c_ident", "c_tri")}
    zc = np.zeros((128, NMETA), np.float32)
    rankA = dict(base, g_wlr=wlr, g_up=gup, g_bias=gb, c_mask=masks(le, gt), **shared,

                 cosT=np.ascontiguousarray(np.concatenate([cosG[:, :NRL], cosG[:, 2 * NRL:]], axis=1)),
                 sinT=np.ascontiguousarray(np.concatenate([sinG[:, :NRL], sinG[:, 2 * NRL:]], axis=1)),
                 selw=np.ascontiguousarray(np.tile(np.array([[0.0, 1.0]], np.float32), (128, 1))))
    rankB = dict(base, g_wlr=wlr_sw, g_up=np.ascontiguousarray(gup[:, ::-1]), g_bias=np.ascontiguousarray(gb[:, ::-1]),

                 c_mask=masks(lt, ge), **shared,
                 cosT=np.ascontiguousarray(np.concatenate([cosG[:, NRL:2 * NRL][:, ::-1], zc + 1.0], axis=1)),
                 sinT=np.ascontiguousarray(np.concatenate([sinG[:, NRL:2 * NRL][:, ::-1], zc], axis=1)),
                 selw=np.ascontiguousarray(np.tile(np.array([[1.0, 0.0]], np.float32), (128, 1))))
    maps = []
    for b in range(B):
        ma = dict(rankA)
        ma["h0"] = np.ascontiguousarray(np.concatenate([x[b, :NRL], meta], axis=0))
        mb = dict(rankB)
        mb["h0"] = np.ascontiguousarray(np.concatenate([x[b, NRL:][::-1], np.zeros((NMETA, D), np.float32)], axis=0))
        maps += [ma, mb]
    return maps


def assemble_pairs(results, B, NRL):
    out = np.empty((B, 2 * NRL, D), np.float32)
    for b in range(B):
        out[b, :NRL] = np.asarray(results[2 * b]["out"]).reshape(NRL, D)
        out[b, NRL:] = np.asarray(results[2 * b + 1]["out"]).reshape(NRL, D)[::-1]
    return out


def make_in_maps(inputs, NR, nb):
    f = lambda a: np.ascontiguousarray(np.asarray(a, dtype=np.float32))
    x = f(inputs["x"])
    meta = f(inputs["meta_tokens"])
    shared = dict(
        prenT=np.ascontiguousarray(f(inputs["pre_norm"]).reshape(4, KT, 128).transpose(0, 2, 1)),
        postn=f(inputs["post_norm"]),
        a_win=f(inputs["attn_w_in"]), a_wout=f(inputs["attn_w_out"]),
        a_qn=np.ascontiguousarray(f(inputs["attn_q_norm"]).T), a_kn=np.ascontiguousarray(f(inputs["attn_k_norm"]).T),
        g_win=f(inputs["gla_w_in"]), g_wout=f(inputs["gla_w_out"]),
        g_up=f(inputs["gla_gk_up"]), g_bias=f(inputs["gla_gk_bias"]), g_on=f(inputs["gla_o_norm"]),
    )
    shared.update(make_consts(NR))
    maps = []
    for b in range(nb):
        m = dict(shared)
        m["h0"] = np.ascontiguousarray(np.concatenate([x[b], meta], axis=0))
        maps.append(m)
    return maps


def kernel(**inputs):
    x = np.asarray(inputs["x"])
    B, SEQ, _ = x.shape
    NRL = SEQ // 2
    nc = Builder(NRL, groups=[[2 * b, 2 * b + 1] for b in range(B)]).build()
    maps = make_in_maps_pairs(inputs, NRL)
    res = run_bass_kernel_spmd(nc, maps, core_ids=list(range(2 * B)))
    return assemble_pairs(res.results, B, NRL)
```

```python
import numpy as np
import ml_dtypes
from contextlib import ExitStack
import concourse.bass as bass
import concourse.mybir as mybir
from concourse.bass_utils import run_bass_kernel_spmd

F32 = mybir.dt.float32
BF16 = mybir.dt.bfloat16
AF = mybir.ActivationFunctionType
ALU = mybir.AluOpType

D = 2048
KT = 16
NMETA = 16
EPS = 1e-6
N_CORES = 4


class Buf:
    __slots__ = ("ap", "writers", "readers")

    def __init__(self, ap=None):
        self.ap = ap
        self.writers = []
        self.readers = []


class Ins:
    __slots__ = ("eng", "fn", "deps", "idx", "needed", "dma", "semkey", "semval", "cnt", "inc")

    def __init__(self, eng, fn, dma=False):
        self.eng = eng
        self.fn = fn
        self.deps = []
        self.idx = -1
        self.needed = False
        self.dma = dma
        self.semkey = None
        self.semval = 0
        self.cnt = 0
        self.inc = 16


class Sched:
    ENGS = ("pe", "act", "dve", "pool", "sp")
    NDMA = {"sp": 24, "pool": 12, "act": 8, "spw": 8}

    def __init__(self, nc):
        self.nc = nc
        self.prog = {e: [] for e in self.ENGS}
        self.dma_rr = {q: 0 for q in self.NDMA}
        self.dma_cnt = {}
        self.dma_last = {}
        self.pending_barrier = {e: None for e in self.ENGS}

    def _track(self, ins, reads, writes):
        deps = {}
        for b in reads:
            for w in b.writers:
                deps[id(w)] = w
        for b in writes:
            for w in b.writers:
                deps[id(w)] = w
            for r in b.readers:
                deps[id(r)] = r
        deps.pop(id(ins), None)
        pb = self.pending_barrier[ins.eng]
        if pb is not None:
            for d in pb:
                deps[id(d)] = d
            self.pending_barrier[ins.eng] = None
        ins.deps = list(deps.values())
        for b in reads:
            b.readers.append(ins)
        for b in writes:
            b.writers = [ins]
            b.readers = []

    def op(self, eng, fn, reads=(), writes=()):
        ins = Ins(eng, fn)
        self._track(ins, reads, writes)
        ins.idx = len(self.prog[eng])
        self.prog[eng].append(ins)
        return ins

    def dma(self, q, out_ap, in_ap, reads=(), writes=(), sempool=None, **kw):
        ins = Ins(q, lambda e: e.dma_start(out=out_ap, in_=in_ap, **kw), dma=True)
        self._track(ins, reads, writes)
        sp_ = sempool or q
        j = self.dma_rr[sp_]
        self.dma_rr[sp_] = (j + 1) % self.NDMA[sp_]
        key = ("dma", sp_, j)
        prev = self.dma_last.get(key)
        if prev is not None:
            ins.deps.append(prev)
        self.dma_cnt[key] = self.dma_cnt.get(key, 0) + 16
        ins.semkey = key
        ins.semval = self.dma_cnt[key]
        self.dma_last[key] = ins
        ins.idx = len(self.prog[q])
        self.prog[q].append(ins)
        return ins

    def cc(self, fn, reads=(), writes=(), sempool="cc"):
        ins = Ins("pool", fn, dma=True)
        ins.inc = 1
        self._track(ins, reads, writes)
        key = ("dma", sempool, 0)
        prev = self.dma_last.get(key)
        if prev is not None:
            ins.deps.append(prev)
        self.dma_cnt[key] = self.dma_cnt.get(key, 0) + 1
        ins.semkey = key
        ins.semval = self.dma_cnt[key]
        self.dma_last[key] = ins
        ins.idx = len(self.prog["pool"])
        self.prog["pool"].append(ins)
        return ins

    def barrier(self):
        deps = []
        for e in self.ENGS:
            if self.prog[e]:
                last = None
                for ins in reversed(self.prog[e]):
                    if not ins.dma:
                        last = ins
                        break
                if last is not None:
                    deps.append(last)
        deps.extend(v for k, v in self.dma_last.items() if k[1] not in ("spw", "ccw"))
        for e in self.ENGS:
            self.pending_barrier[e] = list(deps)

    def emit(self):
        nc = self.nc
        fin = Ins("sp", None)
        fin.deps = list(self.dma_last.values())
        fin.idx = len(self.prog["sp"])
        self.prog["sp"].append(fin)
        for e in self.ENGS:
            seen = {}
            for ins in self.prog[e]:
                best = {}
                for d in ins.deps:
                    if d.dma:
                        key, v = d.semkey, d.semval
                    else:
                        if d.eng == e and e == "pe":
                            continue
                        key, v = d.eng, d.idx + 1
                    if key not in best or best[key][0] < v:
                        best[key] = (v, d)
                kept = []
                for key, (v, d) in best.items():
                    if seen.get(key, 0) >= v:
                        continue
                    seen[key] = v
                    kept.append(d)
                    d.needed = True
                ins.deps = kept
        for e in self.ENGS:
            c = 0
            for ins in self.prog[e]:
                if ins.needed and not ins.dma:
                    c += 1
                    ins.cnt = c
        with ExitStack() as st:
            sems = {}
            for e in ("pe", "act", "dve", "pool", "sp"):
                sems[e] = st.enter_context(nc.semaphore("s_" + e))
            for key in self.dma_cnt:
                sems[key] = st.enter_context(nc.semaphore("d_%s_%d" % (key[1], key[2])))
            block = st.enter_context(nc.Block())
            prog = self.prog

            def run(ename):
                def body(eng):
                    for ins in prog[ename]:
                        waits = {}
                        for d in ins.deps:
                            if d.dma:
                                key, v = d.semkey, d.semval
                            else:
                                key, v = d.eng, d.cnt
                            if waits.get(key, 0) < v:
                                waits[key] = v
                        for key, v in waits.items():
                            eng.wait_ge(sems[key], v)
                        if ins.fn is None:
                            continue
                        r = ins.fn(eng)
                        if ins.dma:
                            r.then_inc(sems[ins.semkey], ins.inc)
                        elif ins.needed:
                            r.then_inc(sems[ename], 1)
                return body

            block.tensor(run("pe"))
            block.scalar(run("act"))
            block.vector(run("dve"))
            block.gpsimd(run("pool"))
            block.sync(run("sp"))


class Rot:
    def __init__(self, bufs):
        self.bufs = bufs
        self.i = 0

    def next(self):
        b = self.bufs[self.i]
        self.i = (self.i + 1) % len(self.bufs)
        return b


class Builder:
    def __init__(self, NR, layers=(0, 1, 2, 3), debug=False, groups=((0, 1),)):
        self.groups = [list(g) for g in groups]
        self.NR = NR
        self.P = NR + NMETA
        self.layers = layers
        self.debug = debug
        P = self.P
        self.tiles = [(i * 128, 128) for i in range(NR // 128)] + [(NR, NMETA)]
        self.NT = len(self.tiles)
        self.chunks = [(c * 512, 512, list(range(4 * c, 4 * c + 4))) for c in range(NR // 512)]
        self.chunks.append((NR, NMETA, [self.NT - 1]))
        self.gchunks = [(NR, NMETA)] + [(i * 64, 64) for i in range(NR // 64)]
        self.NCH = len(self.gchunks)
        nc = bass.Bass("TRN2", target_bir_lowering=False)
        self.nc = nc
        self.S = Sched(nc)
        self.dbufs = {}
        I = lambda name, shape, dt: nc.dram_tensor(name, shape, dt, kind="ExternalInput").ap()
        self.h0 = I("h0", [P, D], F32)
        self.prenT = I("prenT", [4, 128, KT], F32)
        self.postn = I("postn", [4, D], F32)
        self.wfa = {}
        for nm, ncol in (("a_win", 6144), ("a_wout", D), ("g_win", 6144), ("g_wout", D)):
            t_ = I(nm, [2, D, ncol], F32)
            for j in range(2):
                self.wfa[nm, j] = t_[j]
        self.g_wlr = I("g_wlr", [2, D, 32], F32)
        self.wq = []
        self.wq_b = 0
        self.wq_g = 0
        self.a_qn = I("a_qn", [128, 2], F32)
        self.a_kn = I("a_kn", [128, 2], F32)
        self.g_up = I("g_up", [2, 2, 16, 1024], F32)
        self.g_bias = I("g_bias", [2, 2, 1024], F32)
        self.g_on = I("g_on", [2, 512], F32)
        self.cosT = I("cosT", [128, P], F32)
        self.sinT = I("sinT", [128, P], F32)
        self.c_rperm = I("c_rperm", [128, 128], F32)
        self.c_ident = I("c_ident", [128, 128], BF16)
        self.c_tri = I("c_tri", [2, 2, 128, 64], F32)
        self.c_mask = I("c_mask", [2, 64, 256], F32)
        self.selw = I("selw", [128, 2], F32)
        self.out = nc.dram_tensor("out", [NR, D], F32, kind="ExternalOutput").ap()
        kind = "ExternalOutput" if debug else "Internal"
        Sc = lambda name, shape, dt: nc.dram_tensor(name, shape, dt, kind=kind).ap()
        self.h_scr = Sc("h_scr", [P, D], F32)
        self.yT_scr = Sc("yT_scr", [self.NT, 128, D], BF16)
        self.qT_scr = Sc("qT_scr", [16, 128, P], BF16)
        self.kT_loc = [nc.dram_tensor("kT_loc%d" % i, [128, P], BF16) for i in range(8)]
        self.va_loc = [nc.dram_tensor("va_loc%d" % i, [P, 128], BF16) for i in range(8)]
        self.kT_all = [nc.dram_tensor("kT_all%d" % i, [2 * 128, P], BF16) for i in range(8)]
        self.va_all = [nc.dram_tensor("va_all%d" % i, [2 * P, 128], BF16) for i in range(8)]
        self.S_loc = nc.dram_tensor("S_loc", [128, 4096], F32)
        self.S_all = nc.dram_tensor("S_all", [256, 4096], F32)
        self.sgT_scr = Sc("sgT_scr", [16, 128, P], F32)
        self.ogTa_scr = Sc("ogTa_scr", [16, 128, P], BF16)
        self.q32_scr = Sc("q32_scr", [8, 128, P], F32)
        self.k32_scr = Sc("k32_scr", [8, 128, P], F32)
        self.vg_scr = Sc("vg_scr", [P, D], BF16)
        self.sg_scr = Sc("sg_scr", [P, D], F32)
        self.of_scr = Sc("of_scr", [P, D], F32)
        self.ogTg_scr = Sc("ogTg_scr", [self.NCH, 128, 1024], BF16)

    def db(self, *key):
        b = self.dbufs.get(key)
        if b is None:
            b = Buf()
            self.dbufs[key] = b
        return b

    def uname(self, name):
        self.uid = getattr(self, "uid", 0) + 1
        return "%s_u%d" % (name, self.uid)

    def sb(self, st, name, shape, dt, n=1):
        bufs = [Buf(st.enter_context(self.nc.sbuf_tensor(self.uname(name), shape, dt))[:]) for i in range(n)]
        return bufs[0] if n == 1 else Rot(bufs)

    def ps(self, st, name, shape, dt, n=1):
        bufs = [Buf(st.enter_context(self.nc.psum_tensor(self.uname(name), shape, dt))[:]) for i in range(n)]
        return bufs[0] if n == 1 else Rot(bufs)

    def dump(self, name, buf, ap, shape, dt):
        if not self.debug:
            return
        t = self.nc.dram_tensor("dbg_" + name, shape, dt, kind="ExternalOutput").ap()
        self.S.dma("pool", t, ap, reads=[buf])

    def w_bounce(self):
        if self.wq_b >= len(self.wq):
            return
        nm, j = self.wq[self.wq_b]
        self.wq_b += 1
        for r in range(8):
            self.S.dma("sp", self.wb_[nm, j][r * 128:(r + 1) * 128, :], self.wh[nm][j, r * 128:(r + 1) * 128, :],
                       writes=[self.db("wb", nm, j, r)], sempool="spw")

    def w_gather(self):
        if self.wq_g >= len(self.wq):
            return
        nm, j = self.wq[self.wq_g]
        self.wq_g += 1
        src, dst = self.wb_[nm, j], self.wf[nm, j]
        self.S.cc(lambda e: e.collective_compute("AllGather", ALU.bypass, replica_groups=self.groups,
                                                 ins=[src.ap().opt()], outs=[dst.ap().opt()]),
                  reads=[self.db("wb", nm, j, r) for r in range(8)], writes=[self.db("wf", nm, j)], sempool="ccw")

    def w_step(self):
        self.w_gather()
        self.w_bounce()

    def load_consts(self, st):
        S = self.S
        self.ident = self.sb(st, "ident", [128, 128], BF16)
        self.rperm = self.sb(st, "rperm", [128, 128], F32)
        self.ones32 = self.sb(st, "ones32", [128, 128], F32)
        self.onesbf = self.sb(st, "onesbf", [128, 128], BF16)
        self.wpreT = self.sb(st, "wpreT", [128, 4, KT], F32)
        self.qn = self.sb(st, "qn", [128, 2], F32)
        self.kn = self.sb(st, "kn", [128, 2], F32)
        S.dma("sp", self.ident.ap, self.c_ident, writes=[self.ident])
        S.dma("sp", self.rperm.ap, self.c_rperm, writes=[self.rperm])
        S.dma("sp", self.wpreT.ap, self.prenT.rearrange("l p k -> p l k"), writes=[self.wpreT])
        S.dma("sp", self.qn.ap, self.a_qn, writes=[self.qn])
        S.dma("sp", self.kn.ap, self.a_kn, writes=[self.kn])
        S.op("dve", lambda e: e.memset(self.ones32.ap, 1.0), writes=[self.ones32])
        S.op("dve", lambda e: e.memset(self.onesbf.ap, 1.0), writes=[self.onesbf])

    def p1_alloc(self, st):
        self.p1_junk = self.sb(st, "p1junk", [128, D], BF16)
        self.p1_ss = self.sb(st, "p1ss", [128, 1], F32, 2)
        self.p1_yn = self.sb(st, "p1yn", [128, D], BF16, 2)
        self.p1_yT = self.sb(st, "p1yT", [128, KT, 128], BF16, 2)
        self.p1_tp = self.ps(st, "p1tp", [128, 8, 128], BF16, 2)

    def p1_tile(self, hb, ti, li):
        S = self.S
        r0, n = self.tiles[ti]
        ss = self.p1_ss.next()
        yn = self.p1_yn.next()
        yT = self.p1_yT.next()
        junk = self.p1_junk
        S.op("act", lambda e: e.activation(out=junk.ap[:n], in_=hb.ap[:n], func=AF.Square, accum_out=ss.ap[:n]),
             reads=[hb], writes=[junk, ss])
        S.op("act", lambda e: e.activation(out=ss.ap[:n], in_=ss.ap[:n], func=AF.Sqrt, scale=1.0 / D, bias=EPS),
             reads=[ss], writes=[ss])
        S.op("dve", lambda e: e.reciprocal(out=ss.ap[:n], in_=ss.ap[:n]), reads=[ss], writes=[ss])
        S.op("dve", lambda e: e.tensor_scalar(out=yn.ap[:n], in0=hb.ap[:n], scalar1=ss.ap[:n, 0:1], scalar2=None,
                                              op0=ALU.mult), reads=[hb, ss], writes=[yn])
        yield
        for g in range(2):
            tp = self.p1_tp.next()
            for k in range(8):
                kt = g * 8 + k
                S.op("pe", lambda e, k=k, kt=kt, tp=tp: e.transpose(
                    out=tp.ap[:, k, :n], in_=yn.ap[:n, kt * 128:(kt + 1) * 128], identity=self.ident.ap[:n, :n]),
                    reads=[yn, self.ident], writes=[tp])
            S.op("dve", lambda e, g=g, tp=tp: e.tensor_tensor(
                out=yT.ap[:, g * 8:(g + 1) * 8, :n], in0=tp.ap[:, :, :n],
                in1=self.wpreT.ap[:, li, g * 8:(g + 1) * 8].unsqueeze(2).broadcast_to([128, 8, n]), op=ALU.mult),
                reads=[tp, self.wpreT], writes=[yT])
        dst = self.yT_scr[ti].rearrange("p (k t) -> p k t", k=KT)[:, :, :n]
        S.dma("pool", dst, yT.ap[:, :, :n], reads=[yT], writes=[self.db("yT", ti)])

    def p0(self):
        S = self.S
        with ExitStack() as st:
            self.p1_alloc(st)
            hb = self.sb(st, "p0h", [128, D], F32, 3)
            for ti, (r0, n) in enumerate(self.tiles):
                h = hb.next()
                S.dma("sp", h.ap[:n], self.h0[r0:r0 + n, :], writes=[h])
                for _ in self.p1_tile(h, ti, self.layers[0]):
                    pass
            S.barrier()

    def p2(self, li):
        S = self.S
        is_attn = (li % 2 == 0)
        j = li // 2
        wname = "a_win" if is_attn else "g_win"
        wdep = self.db("wf", wname, j)
        Wv = self.wfa[wname, j].rearrange("(k p) n -> p k n", p=128)
        Wlr = self.g_wlr[j].rearrange("(k p) n -> p k n", p=128)
        self.w_step()
        if is_attn:
            jobs = [("aq", h, h * 128, 128) for h in range(16)]
            jobs += [("ak", h, 2048 + h * 128, 128) for h in range(8)]
            jobs += [("av", c, 3072 + c * 256, 256) for c in range(4)]
            jobs += [("ag", f, 4096 + f * 128, 128) for f in range(16)]
        else:
            jobs = [("lr", d, d * 16, 16) for d in range(2)]
            jobs += [("gq", d, d * 128, 128) for d in range(8)]
            jobs += [("gk", d, 1024 + d * 128, 128) for d in range(8)]
            jobs += [("gv", c, 2048 + c * 256, 256) for c in range(8)]
            jobs += [("gg", c, 4096 + c * 256, 256) for c in range(8)]
        nch = len(self.chunks)
        half = nch if self.NT <= 17 else max(1, nch // 2)
        sblocks = [self.chunks[:half], self.chunks[half:]]
        sblocks = [s for s in sblocks if s]
        maxt = max(sum(len(c[2]) for c in s) for s in sblocks)
        maxr = max(sum(c[1] for c in s) for s in sblocks)
        with ExitStack() as st:
            yT = st.enter_context(self.nc.sbuf_tensor(self.uname("p2yT"), [128, maxt, KT, 128], BF16))
            ytb = [Buf(yT[:, i]) for i in range(maxt)]
            wst = self.sb(st, "p2wst", [128, KT, 256], F32, 2)
            wbf = self.sb(st, "p2wbf", [128, KT, 256], BF16, 3)
            pp = self.ps(st, "p2pp", [128, 512], F32, 3)
            if is_attn:
                cos = self.sb(st, "p2cos", [128, maxr], F32)
                sin = self.sb(st, "p2sin", [128, maxr], F32)
                sq = self.sb(st, "p2sq", [128, 512], F32, 2)
                rs = self.sb(st, "p2rs", [128, 512], F32, 2)
                qn = self.sb(st, "p2qn", [128, 512], F32, 2)
                t1 = self.sb(st, "p2t1", [128, 512], F32, 2)
                t2 = self.sb(st, "p2t2", [128, 512], F32, 2)
                qr = self.sb(st, "p2qr", [128, 512], BF16, 3)
                ssp = self.ps(st, "p2ssp", [128, 512], F32, 2)
                rot = self.ps(st, "p2rot", [128, 512], F32, 2)
            o32 = self.sb(st, "p2o32", [128, 512], F32, 3)
            obf = self.sb(st, "p2obf", [128, 256], BF16, 3)
            jobn = 0
            pending = []

            def advance():
                for g_ in list(pending):
                    try:
                        next(g_)
                    except StopIteration:
                        pending.remove(g_)

            def qk_epi(kind, idx, p, n, r0, lo):
                wn = self.qn if kind == "aq" else self.kn
                a_sq, a_rs, a_qn, a_t1, a_t2, a_qr = sq.next(), rs.next(), qn.next(), t1.next(), t2.next(), qr.next()
                a_ssp, a_rot = ssp.next(), rot.next()
                S.op("act", lambda e: e.activation(out=a_sq.ap[:, :n], in_=p.ap[:, :n], func=AF.Square),
                     reads=[p], writes=[a_sq])
                yield
                S.op("pe", lambda e: e.matmul(out=a_ssp.ap[:, :n], lhsT=self.ones32.ap, rhs=a_sq.ap[:, :n],
                                              start=True, stop=True), reads=[a_sq, self.ones32], writes=[a_ssp])
                S.op("act", lambda e: e.activation(out=a_rs.ap[:, :n], in_=a_ssp.ap[:, :n], func=AF.Ln,
                                                   scale=1.0 / 128, bias=EPS), reads=[a_ssp], writes=[a_rs])
                S.op("act", lambda e: e.activation(out=a_rs.ap[:, :n], in_=a_rs.ap[:, :n], func=AF.Exp, scale=-0.5),
                     reads=[a_rs], writes=[a_rs])
                S.op("dve", lambda e: e.scalar_tensor_tensor(
                    out=a_qn.ap[:, :n], in0=p.ap[:, :n], scalar=wn.ap[:, j:j + 1], in1=a_rs.ap[:, :n],
                    op0=ALU.mult, op1=ALU.mult), reads=[p, a_rs, wn], writes=[a_qn])
                yield
                S.op("pe", lambda e: e.matmul(out=a_rot.ap[:, :n], lhsT=self.rperm.ap, rhs=a_qn.ap[:, :n],
                                              start=True, stop=True), reads=[a_qn, self.rperm], writes=[a_rot])
                S.op("pool", lambda e: e.tensor_tensor(out=a_t1.ap[:, :n], in0=a_qn.ap[:, :n], in1=cos.ap[:, lo:lo + n],
                                                       op=ALU.mult), reads=[a_qn, cos], writes=[a_t1])
                S.op("dve", lambda e: e.tensor_tensor(out=a_t2.ap[:, :n], in0=a_rot.ap[:, :n], in1=sin.ap[:, lo:lo + n],
                                                      op=ALU.mult), reads=[a_rot, sin], writes=[a_t2])
                S.op("pool", lambda e: e.tensor_tensor(out=a_qr.ap[:, :n], in0=a_t1.ap[:, :n], in1=a_t2.ap[:, :n],
                                                       op=ALU.add), reads=[a_t1, a_t2], writes=[a_qr])
                if kind == "aq":
                    dst_ap = self.qT_scr[idx, :, r0:r0 + n]
                else:
                    dst_ap = self.kT_loc[idx][:, r0:r0 + n]
                S.dma("pool", dst_ap, a_qr.ap[:, :n], reads=[a_qr], writes=[self.db(kind, idx, r0)])

            for sbk in sblocks:
                tmap = {}
                rbase = sbk[0][0]
                for (r0, n, tl) in sbk:
                    for ti in tl:
                        slot = len(tmap)
                        tmap[ti] = slot
                        tn = self.tiles[ti][1]
                        src = self.yT_scr[ti].rearrange("p (k t) -> p k t", k=KT)[:, :, :tn]
                        S.dma("sp", ytb[slot].ap[:, :, :tn], src, reads=[self.db("yT", ti)], writes=[ytb[slot]])
                if is_attn:
                    tot = sum(c[1] for c in sbk)
                    S.dma("sp", cos.ap[:, :tot], self.cosT[:, rbase:rbase + tot], writes=[cos])
                    S.dma("sp", sin.ap[:, :tot], self.sinT[:, rbase:rbase + tot], writes=[sin])
                def load_w(job):
                    (kind_, idx_, c0_, ncol_) = job
                    ws = wst.next()
                    wb_ = wbf.next()
                    if kind_ == "lr":
                        S.dma("sp", ws.ap[:, :, :ncol_], Wlr[:, :, c0_:c0_ + ncol_], writes=[ws])
                    else:
                        S.dma("sp", ws.ap[:, :, :ncol_], Wv[:, :, c0_:c0_ + ncol_], reads=[wdep], writes=[ws])
                    S.op("act", lambda e: e.activation(out=wb_.ap[:, :, :ncol_], in_=ws.ap[:, :, :ncol_], func=AF.Copy),
                         reads=[ws], writes=[wb_])
                    return wb_

                wnext = load_w(jobs[0])
                for ji, (kind, idx, c0, ncol) in enumerate(jobs):
                    wb = wnext
                    if ji + 1 < len(jobs):
                        wnext = load_w(jobs[ji + 1])
                    if kind in ("av", "gv", "gg"):
                        for (r0, n, tl) in sbk:
                            for ti in tl:
                                tr0, tn = self.tiles[ti]
                                yb = ytb[tmap[ti]]
                                p = pp.next()
                                for kt in range(KT):
                                    S.op("pe", lambda e, p=p, yb=yb, wb=wb, kt=kt, tn=tn, ncol=ncol: e.matmul(
                                        out=p.ap[:tn, :ncol], lhsT=yb.ap[:, kt, :tn], rhs=wb.ap[:, kt, :ncol],
                                        start=(kt == 0), stop=(kt == KT - 1)), reads=[yb, wb], writes=[p])
                                advance()
                                if kind == "gg":
                                    o = o32.next()
                                    S.op("act", lambda e, o=o, p=p, tn=tn, ncol=ncol: e.activation(
                                        out=o.ap[:tn, :ncol], in_=p.ap[:tn, :ncol], func=AF.Silu), reads=[p], writes=[o])
                                    S.dma("pool", self.sg_scr[tr0:tr0 + tn, idx * 256:(idx + 1) * 256], o.ap[:tn, :ncol],
                                          reads=[o], writes=[self.db("sg", ti, idx)])
                                else:
                                    o = obf.next()
                                    S.op("dve", lambda e, o=o, p=p, tn=tn, ncol=ncol: e.tensor_copy(
                                        out=o.ap[:tn, :ncol], in_=p.ap[:tn, :ncol]), reads=[p], writes=[o])
                                    if kind == "av":
                                        for u_ in range(2):
                                            S.dma("pool", self.va_loc[idx * 2 + u_][tr0:tr0 + tn, :],
                                                  o.ap[:tn, u_ * 128:(u_ + 1) * 128], reads=[o],
                                                  writes=[self.db("av", ti, idx * 2 + u_)])
                                    else:
                                        S.dma("pool", self.vg_scr[tr0:tr0 + tn, idx * 256:(idx + 1) * 256], o.ap[:tn, :ncol],
                                              reads=[o], writes=[self.db(kind, ti, idx)])
                        continue
                    M = ncol
                    for ci, (r0, n, tl) in enumerate(sbk):
                        p = pp.next()
                        s0 = tmap[tl[0]]
                        for kt in range(KT):
                            if len(tl) == 1:
                                rhs_fn = lambda kt=kt, s0=s0, n=n: yT[:, s0, kt, :n]
                            else:
                                rhs_fn = lambda kt=kt, s0=s0, tl=tl: yT[:, s0:s0 + len(tl), kt, :]
                            S.op("pe", lambda e, p=p, wb=wb, kt=kt, rhs_fn=rhs_fn, n=n, M=M: e.matmul(
                                out=p.ap[:M, :n], lhsT=wb.ap[:, kt, :M], rhs=rhs_fn(),
                                start=(kt == 0), stop=(kt == KT - 1)),
                                reads=[ytb[tmap[t]] for t in tl] + [wb], writes=[p])
                        lo = r0 - rbase
                        if kind in ("aq", "ak"):
                            pending.append(qk_epi(kind, idx, p, n, r0, lo))
                        advance()
                        if kind == "ag":
                            o = o32.next()
                            S.op("act", lambda e, o=o, p=p, n=n: e.activation(out=o.ap[:, :n], in_=p.ap[:, :n], func=AF.Silu),
                                 reads=[p], writes=[o])
                            S.dma("pool", self.sgT_scr[idx, :, r0:r0 + n], o.ap[:, :n], reads=[o],
                                  writes=[self.db("ag", idx, r0)])
                        elif kind in ("gq", "gk"):
                            o = o32.next()
                            S.op("dve", lambda e, o=o, p=p, n=n: e.tensor_copy(out=o.ap[:, :n], in_=p.ap[:, :n]),
                                 reads=[p], writes=[o])
                            dst = self.q32_scr if kind == "gq" else self.k32_scr
                            S.dma("pool", dst[idx, :, r0:r0 + n], o.ap[:, :n], reads=[o],
                                  writes=[self.db(kind, idx, r0)])
                        elif kind == "lr":
                            lb = self.lrT[idx]
                            S.op("dve", lambda e, lb=lb, p=p, n=n, r0=r0: e.tensor_copy(
                                out=lb.ap[0:16, r0:r0 + n], in_=p.ap[:16, :n]), reads=[p], writes=[lb])
                while pending:
                    advance()
            if is_attn:
                for kv in range(8):
                    S.cc(lambda e, kv=kv: e.collective_compute(
                        "AllGather", ALU.bypass, replica_groups=self.groups,
                        ins=[self.kT_loc[kv].ap().opt()], outs=[self.kT_all[kv].ap().opt()]),
                        reads=[self.db("ak", kv, c[0]) for c in self.chunks], writes=[self.db("kTall", kv)])
                    S.cc(lambda e, kv=kv: e.collective_compute(
                        "AllGather", ALU.bypass, replica_groups=self.groups,
                        ins=[self.va_loc[kv].ap().opt()], outs=[self.va_all[kv].ap().opt()]),
                        reads=[self.db("av", ti, kv) for ti in range(self.NT)], writes=[self.db("vaall", kv)])
            S.barrier()

    def p3_attn(self, li):
        S = self.S
        self.w_step()
        P = self.P
        NT = self.NT
        scale = 128.0 ** -0.5
        with ExitStack() as st:
            nfull = NT - 1
            NKT = 2 * nfull + 1
            ktiles = [(i * 128, 128, i) for i in range(nfull)] + [(self.NR, NMETA, nfull)]
            ktiles += [(P + i * 128, 128, nfull + 1 + i) for i in range(nfull)]
            KTb = self.sb(st, "p3kt", [128, 2 * P], BF16, 2)
            Vb = self.sb(st, "p3v", [128, NKT, 128], BF16, 2)
            qb = self.sb(st, "p3q", [128, 512], BF16, 2)
            sgb = self.sb(st, "p3sg", [128, 512], F32, 2)
            pT = self.sb(st, "p3pT", [128, 512], BF16, 3)
            rec = self.sb(st, "p3rec", [128, 512], F32, 2)
            tt = self.sb(st, "p3tt", [128, 512], F32, 2)
            og = self.sb(st, "p3og", [128, 512], BF16, 2)
            sps = self.ps(st, "p3s", [128, 512], F32, 3)
            ops = self.ps(st, "p3o", [128, 512], F32, 2)
            sums = self.ps(st, "p3sum", [128, 512], F32, 2)
            accb = self.sb(st, "p3acc", [128, 512], F32, 2)
            LA = 2
            groups = []
            for kv in range(8):
                for g in range(2):
                    for ch in self.chunks:
                        groups.append((kv, g, ch))
            units = [(gi, ti) for gi in range(len(groups)) for ti in range(NKT)]
            gstate = {}
            kvstate = {}

            def begin(gi):
                kv, g, (r0, n, tl) = groups[gi]
                if kv not in kvstate:
                    K = KTb.next()
                    V = Vb.next()
                    for r in range(2):
                        i_ = S.dma("sp", K.ap[:, r * P:(r + 1) * P],
                                   self.kT_all[kv][r * 128:(r + 1) * 128, :],
                                   reads=[self.db("kTall", kv)], writes=[K] if r == 0 else [])
                        if r == 1:
                            K.writers.append(i_)
                    vdeps = [self.db("vaall", kv)]
                    S.dma("sp", V.ap[:, 0:nfull, :],
                          self.va_all[kv][0:nfull * 128, :].rearrange("(t p) d -> p t d", p=128),
                          reads=vdeps, writes=[V])
                    i_ = S.dma("sp", V.ap[:NMETA, nfull, :], self.va_all[kv][self.NR:self.NR + NMETA, :],
                               reads=vdeps, writes=[])
                    V.writers.append(i_)
                    i_ = S.dma("sp", V.ap[:, nfull + 1:NKT, :],
                               self.va_all[kv][P:P + nfull * 128, :].rearrange("(t p) d -> p t d", p=128),
                               reads=vdeps, writes=[])
                    V.writers.append(i_)
                    kvstate[kv] = (K, V)
                K, V = kvstate[kv]
                h = kv * 2 + g
                q = qb.next()
                sg = sgb.next()
                S.dma("sp", q.ap[:, :n], self.qT_scr[h, :, r0:r0 + n], reads=[self.db("aq", h, r0)], writes=[q])
                S.dma("sp", sg.ap[:, :n], self.sgT_scr[h, :, r0:r0 + n], reads=[self.db("ag", h, r0)], writes=[sg])
                gstate[gi] = dict(K=K, V=V, q=q, sg=sg, o=ops.next(), s=sums.next(), acc=accb.next(), h=h, r0=r0, n=n, pt={})

            def front(gi, ti):
                st_ = gstate[gi]
                K, q, n = st_["K"], st_["q"], st_["n"]
                k0, nk, vs = ktiles[ti]
                sp_ = sps.next()
                p_ = pT.next()
                st_["pt"][ti] = p_
                S.op("pe", lambda e: e.matmul(out=sp_.ap[:nk, :n], lhsT=K.ap[:, k0:k0 + nk], rhs=q.ap[:, :n],
                                              start=True, stop=True), reads=[K, q], writes=[sp_])
                S.op("act", lambda e: e.activation(out=p_.ap[:nk, :n], in_=sp_.ap[:nk, :n], func=AF.Exp, scale=scale),
                     reads=[sp_], writes=[p_])

            def back(gi, ti):
                st_ = gstate[gi]
                V, n, o_ps, s_ps = st_["V"], st_["n"], st_["o"], st_["s"]
                k0, nk, vs = ktiles[ti]
                p_ = st_["pt"].pop(ti)
                S.op("pe", lambda e: e.matmul(out=o_ps.ap[:, :n], lhsT=V.ap[:nk, vs, :], rhs=p_.ap[:nk, :n],
                                              start=(ti == 0), stop=(ti == NKT - 1)), reads=[V, p_], writes=[o_ps])
                acc = st_["acc"]
                if ti % 3 == 0 and nk == 128:
                    if ti == 0:
                        S.op("dve", lambda e: e.tensor_copy(out=acc.ap[:, :n], in_=p_.ap[:, :n]), reads=[p_], writes=[acc])
                    else:
                        S.op("dve", lambda e: e.tensor_tensor(out=acc.ap[:, :n], in0=acc.ap[:, :n], in1=p_.ap[:, :n],
                                                              op=ALU.add), reads=[acc, p_], writes=[acc])
                else:
                    S.op("pe", lambda e: e.matmul(out=s_ps.ap[:, :n], lhsT=self.onesbf.ap[:nk, :], rhs=p_.ap[:nk, :n],
                                                  start=(ti == 1), stop=False), reads=[self.onesbf, p_], writes=[s_ps])
                if ti == NKT - 1:
                    S.op("pe", lambda e: e.matmul(out=s_ps.ap[:, :n], lhsT=self.ones32.ap, rhs=acc.ap[:, :n],
                                                  start=False, stop=True), reads=[self.ones32, acc], writes=[s_ps])
                    sg, h, r0 = st_["sg"], st_["h"], st_["r0"]
                    r_, t_, og_ = rec.next(), tt.next(), og.next()
                    S.op("dve", lambda e: e.reciprocal(out=r_.ap[:, :n], in_=s_ps.ap[:, :n]), reads=[s_ps], writes=[r_])
                    S.op("dve", lambda e: e.tensor_tensor(out=t_.ap[:, :n], in0=o_ps.ap[:, :n], in1=r_.ap[:, :n], op=ALU.mult),
                         reads=[o_ps, r_], writes=[t_])
                    S.op("pool", lambda e: e.tensor_tensor(out=og_.ap[:, :n], in0=t_.ap[:, :n], in1=sg.ap[:, :n], op=ALU.mult),
                         reads=[t_, sg], writes=[og_])
                    S.dma("pool", self.ogTa_scr[h, :, r0:r0 + n], og_.ap[:, :n], reads=[og_],
                          writes=[self.db("ogTa", h, r0)])
                    del gstate[gi]

            for ui in range(len(units) + LA):
                if ui < len(units):
                    gi, ti = units[ui]
                    if ti == 0:
                        begin(gi)
                    front(gi, ti)
                if ui >= LA:
                    gi, ti = units[ui - LA]
                    back(gi, ti)
            S.barrier()

    def p4(self, li, pos):
        S = self.S
        is_attn = (li % 2 == 0)
        j = li // 2
        first = (pos == 0)
        last = (pos == len(self.layers) - 1)
        woname = "a_wout" if is_attn else "g_wout"
        wodep = self.db("wf", woname, j)
        Wo = self.wfa[woname, j].rearrange("(k p) n -> p k n", p=128)
        self.w_step()
        with ExitStack() as st:
            if not last:
                self.p1_alloc(st)
            wo = st.enter_context(self.nc.sbuf_tensor(self.uname("p4wo"), [128, KT, D], BF16))
            wob = [Buf(wo[:, 2 * i:2 * i + 2, :]) for i in range(8)]
            wst = self.sb(st, "p4wst", [128, 2, D], F32, 2)
            wpost = self.sb(st, "p4wpost", [128, D], F32)
            ogt = self.sb(st, "p4og", [128, KT, 128], BF16, 2)
            hb = self.sb(st, "p4h", [128, D], F32, 2)
            tb = self.sb(st, "p4t", [128, D], F32, 2)
            hn = self.sb(st, "p4hn", [128, D], F32, 2)
            ss4 = self.sb(st, "p4ss4", [128, 4], F32, 2)
            ss = self.sb(st, "p4ss", [128, 1], F32, 2)
            junk = self.sb(st, "p4junk", [128, 512], BF16)
            yps = self.ps(st, "p4y", [128, 512], F32, 6 if not last else 8)
            S.dma("sp", wpost.ap, self.postn[li:li + 1, :].broadcast_to([128, D]), writes=[wpost])
            for i in range(8):
                w = wst.next()
                S.dma("sp", w.ap, Wo[:, 2 * i:2 * i + 2, :], reads=[wodep], writes=[w])
                if i % 2 == 0:
                    S.op("act", lambda e, w=w, i=i: e.activation(out=wob[i].ap, in_=w.ap, func=AF.Copy),
                         reads=[w], writes=[wob[i]])
                else:
                    S.op("pool", lambda e, w=w, i=i: e.tensor_copy(out=wob[i].ap, in_=w.ap), reads=[w], writes=[wob[i]])
            prev_p1 = None
            for ti, (r0, n) in enumerate(self.tiles):
                og = ogt.next()
                if is_attn:
                    deps = [self.db("ogTa", h, c[0]) for h in range(16) for c in self.chunks if c[0] <= r0 < c[0] + c[1]]
                    S.dma("sp", og.ap[:, :, :n], self.ogTa_scr[:, :, r0:r0 + n].rearrange("f p t -> p f t"),
                          reads=deps, writes=[og])
                    lfn = lambda ft, og=og, n=n: og.ap[:, ft, :n]
                elif n == 128:
                    ci = 1 + r0 // 64
                    S.dma("sp", og.ap[:, :, 0:64], self.ogTg_scr[ci].rearrange("p (f t) -> p f t", f=KT),
                          reads=[self.db("ogTg", ci)], writes=[og])
                    i2 = S.dma("sp", og.ap[:, :, 64:128], self.ogTg_scr[ci + 1].rearrange("p (f t) -> p f t", f=KT),
                               reads=[self.db("ogTg", ci + 1)], writes=[])
                    og.writers.append(i2)
                    lfn = lambda ft, og=og: og.ap[:, ft, :]
                else:
                    S.dma("sp", og.ap[:, :, :n], self.ogTg_scr[0].rearrange("p (f t) -> p f t", f=KT)[:, :, :n],
                          reads=[self.db("ogTg", 0)], writes=[og])
                    lfn = lambda ft, og=og, n=n: og.ap[:, ft, :n]
                h = hb.next()
                hsrc = self.h0 if first else self.h_scr
                S.dma("sp", h.ap[:n], hsrc[r0:r0 + n, :], reads=[] if first else [self.db("h", ti)], writes=[h])
                ys = []
                s4 = ss4.next()
                for nb in range(4):
                    y = yps.next()
                    ys.append(y)
                    for ft in range(KT):
                        S.op("pe", lambda e, y=y, ft=ft, nb=nb, lfn=lfn, n=n: e.matmul(
                            out=y.ap[:n, :], lhsT=lfn(ft), rhs=wo[:, ft, nb * 512:(nb + 1) * 512],
                            start=(ft == 0), stop=(ft == KT - 1)), reads=[og, wob[ft // 2]], writes=[y])
                    S.op("act", lambda e, y=y, nb=nb, s4=s4, n=n: e.activation(
                        out=junk.ap[:n], in_=y.ap[:n], func=AF.Square, accum_out=s4.ap[:n, nb:nb + 1]),
                        reads=[y], writes=[junk, s4] if nb == 0 else [junk], )
                    if nb > 0:
                        s4.writers.append(S.prog["act"][-1])
                if prev_p1 is not None:
                    next(prev_p1, None)
                    prev_p1 = None
                s1 = ss.next()
                S.op("dve", lambda e, s1=s1, s4=s4, n=n: e.tensor_reduce(
                    out=s1.ap[:n], in_=s4.ap[:n], axis=mybir.AxisListType.X, op=ALU.add), reads=[s4], writes=[s1])
                S.op("act", lambda e, s1=s1, n=n: e.activation(out=s1.ap[:n], in_=s1.ap[:n], func=AF.Sqrt, scale=1.0 / D, bias=EPS),
                     reads=[s1], writes=[s1])
                S.op("dve", lambda e, s1=s1, n=n: e.reciprocal(out=s1.ap[:n], in_=s1.ap[:n]), reads=[s1], writes=[s1])
                t = tb.next()
                tparts = []
                for nb in range(4):
                    S.op("dve", lambda e, t=t, y=ys[nb], s1=s1, nb=nb, n=n: e.scalar_tensor_tensor(
                        out=t.ap[:n, nb * 512:(nb + 1) * 512], in0=y.ap[:n], scalar=s1.ap[:n, 0:1],
                        in1=wpost.ap[:n, nb * 512:(nb + 1) * 512], op0=ALU.mult, op1=ALU.mult),
                        reads=[ys[nb], s1, wpost], writes=[t] if nb == 0 else [])
                    if nb > 0:
                        t.writers.append(S.prog["dve"][-1])
                hnew = hn.next()
                S.op("pool", lambda e, hnew=hnew, t=t, h=h, n=n: e.tensor_tensor(
                    out=hnew.ap[:n], in0=t.ap[:n], in1=h.ap[:n], op=ALU.add), reads=[t, h], writes=[hnew])
                if last:
                    if r0 < self.NR:
                        S.dma("pool", self.out[r0:r0 + n, :], hnew.ap[:n], reads=[hnew], writes=[self.db("out", ti)])
                else:
                    S.dma("pool", self.h_scr[r0:r0 + n, :], hnew.ap[:n], reads=[hnew], writes=[self.db("h", ti)])
                    prev_p1 = self.p1_tile(hnew, ti, self.layers[pos + 1])
                    next(prev_p1)
            if prev_p1 is not None:
                next(prev_p1, None)
            S.barrier()

    def p3_gla(self, li):
        S = self.S
        self.w_step()
        j = li // 2
        NCH = self.NCH
        with ExitStack() as st:
            up = [self.sb(st, "gup%d" % d, [128, 1024], F32) for d in range(2)]
            tri = [[self.sb(st, "gtri%d_%d" % (d, k), [128, 64], F32) for k in range(2)] for d in range(2)]
            mask = [self.sb(st, "gmask%d" % d, [64, 4, 64], F32) for d in range(2)]
            wo_bc = self.sb(st, "gwobc", [128, 512], F32)
            for d in range(2):
                S.op("pool", lambda e, d=d: e.memset(up[d].ap, 0.0), writes=[up[d]])
                S.dma("sp", up[d].ap[0:16, :], self.g_up[j, d], writes=[up[d]])
                S.dma("sp", up[d].ap[32:33, :], self.g_bias[j, d:d + 1, :], writes=[up[d]])
                for k in range(2):
                    S.dma("sp", tri[d][k].ap, self.c_tri[d, k], writes=[tri[d][k]])
                S.dma("sp", mask[d].ap, self.c_mask[d].rearrange("s (h c) -> s h c", h=4), writes=[mask[d]])
            S.dma("sp", wo_bc.ap, self.g_on[j:j + 1, :].broadcast_to([128, 512]), writes=[wo_bc])
            Sst = st.enter_context(self.nc.sbuf_tensor(self.uname("gS"), [128, 8, 512], F32))
            Sbf_t = st.enter_context(self.nc.sbuf_tensor(self.uname("gSbf"), [128, 8, 512], BF16))
            Sb = [Buf(Sst[:, i, :]) for i in range(8)]
            Sbf = [Buf(Sbf_t[:, i, :]) for i in range(8)]
            q32 = self.sb(st, "gq32", [128, 8, 64], F32, 2)
            k32 = self.sb(st, "gk32", [128, 8, 64], F32, 2)
            vb = self.sb(st, "gv", [64, D], BF16, 4)
            ofb = self.sb(st, "gof", [64, D], F32, 2)
            sgb = self.sb(st, "gsg", [64, D], F32, 2)
            eb = self.sb(st, "ge", [64, 1024], F32, 2)
            gpb = self.sb(st, "ggp", [128, 1024], F32, 2)
            for b_ in gpb.bufs:
                S.op("pool", lambda e, b_=b_: e.memset(b_.ap, 0.0), writes=[b_])
            E1b = self.sb(st, "gE1", [128, 8, 64], F32, 3)
            E2b = self.sb(st, "gE2", [128, 8, 64], F32, 2)
            qdb = self.sb(st, "gqd", [128, 8, 64], BF16, 3)
            ke32b = self.sb(st, "gke32", [128, 8, 64], F32, 2)
            kib = self.sb(st, "gki", [128, 8, 64], BF16, 2)
            keTb = self.sb(st, "gkeT", [128, 8, 64], BF16, 2)
            kendb = self.sb(st, "gkend", [64, 8, 128], BF16, 2)
            pTb = self.sb(st, "gpT", [64, 4, 64], BF16, 2)
            osb = self.sb(st, "gos", [64, D], F32, 2)
            ss4b = self.sb(st, "gss4", [64, 4], F32, 2)
            ogb = self.sb(st, "gog", [64, D], BF16, 2)
            ogTb = self.sb(st, "gogT", [128, KT, 64], BF16, 2)
            junk = self.sb(st, "gjunk", [64, 512], BF16)
            zb = self.ps(st, "gz", [128, 512], F32, 2)
            scp = self.ps(st, "gsc", [64, 8, 64], F32)
            tpx = self.ps(st, "gtpx", [128, 1024], BF16)
            tpk = tpx.ap.rearrange("p (a b) -> p a b", a=8)
            tpo = tpx.ap.rearrange("p (a b) -> p a b", a=KT)
            ops = self.ps(st, "go", [64, 512], F32, 2)
            upd = self.ps(st, "gupd", [128, 512], F32, 2)
            xt = self.sb(st, "gxt", [128, 512], F32, 4)
            selw = self.sb(st, "gselw", [128, 2], F32)
            S.dma("sp", selw.ap, self.selw, writes=[selw])
            for dr in range(2):
                order = list(range(NCH)) if dr == 0 else list(range(NCH - 1, -1, -1))
                if dr == 1:
                    S.dma("pool", self.S_loc[:, :], Sst[:, :, :].rearrange("p d e -> p (d e)"), reads=Sb,
                          writes=[self.db("Sloc")])
                    S.cc(lambda e: e.collective_compute("AllGather", ALU.bypass, replica_groups=self.groups,
                                                        ins=[self.S_loc.ap().opt()], outs=[self.S_all.ap().opt()]),
                         reads=[self.db("Sloc")], writes=[self.db("Sall")])
                    for dt in range(8):
                        x0, x1 = xt.next(), xt.next()
                        S.dma("sp", x0.ap, self.S_all[0:128, dt * 512:(dt + 1) * 512], reads=[self.db("Sall")], writes=[x0])
                        S.dma("sp", x1.ap, self.S_all[128:256, dt * 512:(dt + 1) * 512], reads=[self.db("Sall")], writes=[x1])
                        S.op("dve", lambda e, dt=dt, x0=x0: e.tensor_scalar(
                            out=Sb[dt].ap, in0=x0.ap, scalar1=selw.ap[:, 0:1], scalar2=None, op0=ALU.mult),
                            reads=[x0, selw], writes=[Sb[dt]])
                        S.op("dve", lambda e, dt=dt, x1=x1: e.scalar_tensor_tensor(
                            out=Sb[dt].ap, in0=x1.ap, scalar=selw.ap[:, 1:2], in1=Sb[dt].ap, op0=ALU.mult, op1=ALU.add),
                            reads=[x1, selw, Sb[dt]], writes=[Sb[dt]])
                        S.op("act", lambda e, dt=dt: e.activation(out=Sbf[dt].ap, in_=Sb[dt].ap, func=AF.Copy),
                             reads=[Sb[dt]], writes=[Sbf[dt]])
                def chunk_gen(step, ci, dr=dr):
                    r0, n = self.gchunks[ci]
                    firstc = (dr == 0 and step == 0)
                    lastc = (dr == 1 and step == NCH - 1)
                    up_d, mk, lr = up[dr], mask[dr], self.lrT[dr]
                    trk = tri[dr][0 if n == 64 else 1]
                    lastcol = (n - 1) if dr == 0 else 0
                    cidx = [c[0] for c in self.chunks if c[0] <= r0 < c[0] + c[1]][0]
                    tidx = [t for t, (a, b) in enumerate(self.tiles) if a <= r0 < a + b][0]
                    q3, k3, v = q32.next(), k32.next(), vb.next()
                    S.dma("sp", q3.ap[:, :, :n], self.q32_scr[:, :, r0:r0 + n].rearrange("d p t -> p d t"),
                          reads=[self.db("gq", d, cidx) for d in range(8)], writes=[q3])
                    S.dma("sp", k3.ap[:, :, :n], self.k32_scr[:, :, r0:r0 + n].rearrange("d p t -> p d t"),
                          reads=[self.db("gk", d, cidx) for d in range(8)], writes=[k3])
                    S.dma("sp", v.ap[:n], self.vg_scr[r0:r0 + n, :], reads=[self.db("gv", tidx, c) for c in range(8)], writes=[v])
                    e_, gp = eb.next(), gpb.next()
                    for hf in range(2):
                        z = zb.next()
                        S.op("pe", lambda e, z=z, hf=hf: e.matmul(
                            out=z.ap[:n, :], lhsT=lr.ap[:, r0:r0 + n], rhs=up_d.ap[:, hf * 512:(hf + 1) * 512],
                            start=True, stop=True), reads=[lr, up_d], writes=[z])
                        S.op("act", lambda e, z=z, hf=hf: e.activation(
                            out=e_.ap[:n, hf * 512:(hf + 1) * 512], in_=z.ap[:n, :], func=AF.Exp, scale=-1.0),
                            reads=[z], writes=[e_] if hf == 0 else [])
                        if hf == 1:
                            e_.writers.append(S.prog["act"][-1])
                    S.op("act", lambda e: e.activation(out=gp.ap[:n], in_=e_.ap[:n], func=AF.Ln, bias=1.0),
                         reads=[e_], writes=[gp])
                    yield
                    bT = zb.next()
                    bv = bT.ap.rearrange("p (d c) -> p d c", d=8)
                    for dt in range(8):
                        S.op("pe", lambda e, dt=dt: e.matmul(
                            out=bv[:, dt, :n], lhsT=gp.ap[:, dt * 128:(dt + 1) * 128], rhs=trk.ap[:, :n],
                            start=True, stop=True), reads=[gp, trk], writes=[bT])
                    E1, E2 = E1b.next(), E2b.next()
                    S.op("act", lambda e: e.activation(out=E1.ap[:, :, :n], in_=bv[:, :, :n], func=AF.Exp),
                         reads=[bT], writes=[E1])
                    S.op("act", lambda e: e.activation(out=E2.ap[:, :, :n], in_=bv[:, :, :n], func=AF.Exp, scale=-1.0),
                         reads=[bT], writes=[E2])
                    qd, ke32, ki, keT = qdb.next(), ke32b.next(), kib.next(), keTb.next()
                    S.op("dve", lambda e: e.scalar_tensor_tensor(
                        out=qd.ap[:, :, :n], in0=q3.ap[:, :, :n], scalar=0.0625, in1=E1.ap[:, :, :n],
                        op0=ALU.mult, op1=ALU.mult), reads=[q3, E1], writes=[qd])
                    S.op("dve", lambda e: e.tensor_tensor(
                        out=ke32.ap[:, :, :n], in0=k3.ap[:, :, :n], in1=E2.ap[:, :, :n], op=ALU.mult),
                        reads=[k3, E2], writes=[ke32])
                    S.op("act", lambda e: e.activation(out=ki.ap[:, :, :n], in_=ke32.ap[:, :, :n], func=AF.Copy),
                         reads=[ke32], writes=[ki])
                    if not lastc:
                        S.op("dve", lambda e: e.tensor_tensor(
                            out=keT.ap[:, :, :n], in0=ke32.ap[:, :, :n],
                            in1=E1.ap[:, :, lastcol:lastcol + 1].broadcast_to([128, 8, n]), op=ALU.mult),
                            reads=[ke32, E1], writes=[keT])
                    yield
                    if dr == 1:
                        of, sg = ofb.next(), sgb.next()
                        S.dma("sp", of.ap[:n], self.of_scr[r0:r0 + n, :], reads=[self.db("of", ci)], writes=[of])
                        S.dma("sp", sg.ap[:n], self.sg_scr[r0:r0 + n, :], reads=[self.db("sg", tidx, c) for c in range(8)],
                              writes=[sg])
                        S.op("pool", lambda e: e.tensor_tensor(
                            out=sg.ap[:n].rearrange("p (h e) -> p h e", h=4), in0=sg.ap[:n].rearrange("p (h e) -> p h e", h=4),
                            in1=wo_bc.ap[:n].unsqueeze(1).broadcast_to([n, 4, 512]), op=ALU.mult),
                            reads=[sg, wo_bc], writes=[sg])
                    if not lastc:
                        kend = kendb.next()
                        for dt in range(8):
                            S.op("pe", lambda e, dt=dt: e.transpose(
                                out=tpk[:n, dt, :], in_=keT.ap[:, dt, :n], identity=self.ident.ap),
                                reads=[keT, self.ident], writes=[tpx])
                        S.op("act", lambda e: e.activation(out=kend.ap[:n], in_=tpk[:n], func=AF.Copy),
                             reads=[tpx], writes=[kend])
                    for hh in range(4):
                        for u in range(2):
                            dt = hh * 2 + u
                            S.op("pe", lambda e, hh=hh, dt=dt, u=u: e.matmul(
                                out=scp.ap[:n, hh, :n], lhsT=ki.ap[:, dt, :n], rhs=qd.ap[:, dt, :n],
                                start=(u == 0), stop=(u == 1)), reads=[ki, qd], writes=[scp])
                    pT = pTb.next()
                    S.op("dve", lambda e: e.tensor_tensor(
                        out=pT.ap[:n, :, :n], in0=scp.ap[:n, 0:4, :n], in1=mk.ap[:n, :, :n], op=ALU.mult),
                        reads=[scp, mk], writes=[pT])
                    yield
                    if dr == 1:
                        osum, s4, ogx = osb.next(), ss4b.next(), ogb.next()
                    else:
                        osum = osb.next()
                    for hh in range(4):
                        o = ops.next()
                        S.op("pe", lambda e, o=o, hh=hh: e.matmul(
                            out=o.ap[:n, :], lhsT=pT.ap[:n, hh, :n], rhs=v.ap[:n, hh * 512:(hh + 1) * 512],
                            start=True, stop=firstc), reads=[pT, v], writes=[o])
                        if not firstc:
                            for u in range(2):
                                dt = hh * 2 + u
                                S.op("pe", lambda e, o=o, dt=dt, u=u: e.matmul(
                                    out=o.ap[:n, :], lhsT=qd.ap[:, dt, :n], rhs=Sbf[dt].ap,
                                    start=False, stop=(u == 1)), reads=[qd, Sbf[dt]], writes=[o])
                        hs = slice(hh * 512, (hh + 1) * 512)
                        if dr == 0:
                            S.op("act", lambda e, o=o, hs=hs: e.activation(
                                out=osum.ap[:n, hs], in_=o.ap[:n, :], func=AF.Copy),
                                reads=[o], writes=[osum] if hh == 0 else [])
                            if hh > 0:
                                osum.writers.append(S.prog["act"][-1])
                        else:
                            S.op("dve", lambda e, o=o, hs=hs: e.tensor_tensor(
                                out=osum.ap[:n, hs], in0=o.ap[:n, :], in1=of.ap[:n, hs], op=ALU.add),
                                reads=[o, of], writes=[osum] if hh == 0 else [])
                            if hh > 0:
                                osum.writers.append(S.prog["dve"][-1])
                            S.op("act", lambda e, hs=hs, hh=hh: e.activation(
                                out=junk.ap[:n], in_=osum.ap[:n, hs], func=AF.Square, accum_out=s4.ap[:n, hh:hh + 1]),
                                reads=[osum], writes=[junk, s4] if hh == 0 else [junk])
                            if hh > 0:
                                s4.writers.append(S.prog["act"][-1])
                        if not lastc:
                            for u in range(2):
                                dt = hh * 2 + u
                                ub = upd.next()
                                S.op("pe", lambda e, dt=dt, hh=hh, ub=ub: e.matmul(
                                    out=ub.ap, lhsT=kend.ap[:n, dt, :], rhs=v.ap[:n, hh * 512:(hh + 1) * 512],
                                    start=True, stop=True), reads=[kend, v], writes=[ub])
                                if firstc:
                                    S.op("dve", lambda e, dt=dt, ub=ub: e.tensor_copy(out=Sb[dt].ap, in_=ub.ap),
                                         reads=[ub], writes=[Sb[dt]])
                                else:
                                    S.op("dve", lambda e, dt=dt, ub=ub: e.scalar_tensor_tensor(
                                        out=Sb[dt].ap, in0=Sb[dt].ap, scalar=E1.ap[:, dt, lastcol:lastcol + 1], in1=ub.ap,
                                        op0=ALU.mult, op1=ALU.add), reads=[Sb[dt], E1, ub], writes=[Sb[dt]])
                                if dt % 2 == 0:
                                    S.op("act", lambda e, dt=dt: e.activation(out=Sbf[dt].ap, in_=Sb[dt].ap, func=AF.Copy),
                                         reads=[Sb[dt]], writes=[Sbf[dt]])
                                else:
                                    S.op("dve", lambda e, dt=dt: e.tensor_copy(out=Sbf[dt].ap, in_=Sb[dt].ap),
                                         reads=[Sb[dt]], writes=[Sbf[dt]])
                    if dr == 0:
                        S.dma("pool", self.of_scr[r0:r0 + n, :], osum.ap[:n], reads=[osum], writes=[self.db("of", ci)])
                    else:
                        S.op("act", lambda e: e.activation(out=s4.ap[:n], in_=s4.ap[:n], func=AF.Sqrt,
                                                           scale=1.0 / 512, bias=EPS), reads=[s4], writes=[s4])
                        S.op("dve", lambda e: e.reciprocal(out=s4.ap[:n], in_=s4.ap[:n]), reads=[s4], writes=[s4])
                        for hh in range(4):
                            hs = slice(hh * 512, (hh + 1) * 512)
                            S.op("dve", lambda e, hs=hs, hh=hh: e.scalar_tensor_tensor(
                                out=ogx.ap[:n, hs], in0=osum.ap[:n, hs], scalar=s4.ap[:n, hh:hh + 1], in1=sg.ap[:n, hs],
                                op0=ALU.mult, op1=ALU.mult), reads=[osum, s4, sg], writes=[ogx] if hh == 0 else [])
                            if hh > 0:
                                ogx.writers.append(S.prog["dve"][-1])
                        for ft in range(KT):
                            S.op("pe", lambda e, ft=ft: e.transpose(
                                out=tpo[:, ft, :n], in_=ogx.ap[:n, ft * 128:(ft + 1) * 128], identity=self.ident.ap[:n, :n]),
                                reads=[ogx, self.ident], writes=[tpx])
                        ogT = ogTb.next()
                        S.op("act", lambda e: e.activation(out=ogT.ap[:, :, :n], in_=tpo[:, :, :n], func=AF.Copy),
                             reads=[tpx], writes=[ogT])
                        S.dma("pool", self.ogTg_scr[ci].rearrange("p (f t) -> p f t", f=KT)[:, :, :n], ogT.ap[:, :, :n],
                              reads=[ogT], writes=[self.db("ogTg", ci)])

                gens = []

                def tick():
                    for g_ in list(gens):
                        try:
                            next(g_)
                        except StopIteration:
                            gens.remove(g_)

                for step, ci in enumerate(order):
                    gens.append(chunk_gen(step, ci))
                    tick()
                while gens:
                    tick()
            S.barrier()

    def build(self):
        with ExitStack() as st:
            self.load_consts(st)
            self.w_bounce()
            self.w_gather()
            self.w_bounce()
            self.p0()
            for pos, li in enumerate(self.layers):
                if li % 2 == 0:
                    self.p2(li)
                    self.p3_attn(li)
                else:
                    with ExitStack() as st2:
                        self.lrT = [self.sb(st2, "lrT%d_%d" % (li, d), [128, self.P], F32) for d in range(2)]
                        for lb in self.lrT:
                            self.S.op("pool", lambda e, lb=lb: e.memset(lb.ap, 0.0), writes=[lb])
                            self.S.op("pool", lambda e, lb=lb: e.memset(lb.ap[32:33, :], 1.0), writes=[lb])
                        self.p2(li)
                        self.p3_gla(li)
                    self.S.barrier()
                self.p4(li, pos)
            self.S.emit()
        return self.nc


def make_consts(NR):
    P = NR + NMETA
    t = np.arange(NR)
    row = (t // 64).astype(np.float32)
    col = (t % 64).astype(np.float32)
    inv = (np.float32(10000.0) ** (-np.arange(0, 64, 2, dtype=np.float32) / np.float32(64))).astype(np.float32)
    ang = np.zeros((P, 128), np.float32)
    ar = row[:, None] * inv[None]
    ac = col[:, None] * inv[None]
    ang[:NR, 0:32] = ar
    ang[:NR, 32:64] = ar
    ang[:NR, 64:96] = ac
    ang[:NR, 96:128] = ac
    cos = np.cos(ang).astype(np.float32)
    sin = np.sin(ang).astype(np.float32)
    sgn = np.ones(128, np.float32)
    sgn[0:32] = -1
    sgn[64:96] = -1
    sin = sin * sgn[None]
    rperm = np.zeros((128, 128), np.float32)
    for m in range(128):
        src = m + 32 if (m % 64) < 32 else m - 32
        rperm[src, m] = 1
    jj = np.arange(64)
    tri = np.zeros((2, 2, 128, 64), np.float32)
    tri[0, 0, :64] = np.where(jj[:, None] <= jj[None, :], -1.0 / 16, 0.0)
    tri[1, 0, :64] = np.where(jj[:, None] >= jj[None, :], -1.0 / 16, 0.0)
    tri[:, 1, :16, :16] = tri[:, 0, :16, :16]
    mask = np.zeros((2, 64, 4, 64), np.float32)
    mask[0] = np.where(jj[:, None] <= jj[None, :], 1.0, 0.0)[:, None, :]
    mask[1] = np.where(jj[:, None] > jj[None, :], 1.0, 0.0)[:, None, :]
    return dict(cosT=np.ascontiguousarray(cos.T), sinT=np.ascontiguousarray(sin.T), c_rperm=rperm,
                c_ident=np.eye(128).astype(ml_dtypes.bfloat16), c_tri=tri,
                c_mask=np.ascontiguousarray(mask.reshape(2, 64, 256)))


def make_in_maps_pairs(inputs, NRL):
    f = lambda a: np.ascontiguousarray(np.asarray(a, dtype=np.float32))
    x = f(inputs["x"])
    meta = f(inputs["meta_tokens"])
    B, SEQ, _ = x.shape
    assert SEQ == 2 * NRL
    base = dict(
        prenT=np.ascontiguousarray(f(inputs["pre_norm"]).reshape(4, KT, 128).transpose(0, 2, 1)),
        postn=f(inputs["post_norm"]),
        a_qn=np.ascontiguousarray(f(inputs["attn_q_norm"]).T), a_kn=np.ascontiguousarray(f(inputs["attn_k_norm"]).T),
        g_on=f(inputs["gla_o_norm"]),
    )
    gw = f(inputs["gla_w_in"])
    base.update(a_win=f(inputs["attn_w_in"]), a_wout=f(inputs["attn_w_out"]),
                g_win=np.ascontiguousarray(gw[:, :, :6144]), g_wout=f(inputs["gla_w_out"]))
    wlr = np.ascontiguousarray(gw[:, :, 6144:6176])
    wlr_sw = np.ascontiguousarray(np.concatenate([gw[:, :, 6160:6176], gw[:, :, 6144:6160]], axis=2))
    gup = f(inputs["gla_gk_up"])
    gb = f(inputs["gla_gk_bias"])
    cst = make_consts(2 * NRL)
    cosG, sinG = cst["cosT"], cst["sinT"]
    jj = np.arange(64)
    def masks(m0, m1):
        m = np.zeros((2, 64, 4, 64), np.float32)
        m[0] = m0[:, None, :]
        m[1] = m1[:, None, :]
        return np.ascontiguousarray(m.reshape(2, 64, 256))
    le = (jj[:, None] <= jj[None, :]).astype(np.float32)
    lt = (jj[:, None] < jj[None, :]).astype(np.float32)
    gt = (jj[:, None] > jj[None, :]).astype(np.float32)
    ge = (jj[:, None] >= jj[None, :]).astype(np.float32)
    shared = {k: cst[k] for k in ("c_rperm", "c_ident", "c_tri")}
    zc = np.zeros((128, NMETA), np.float32)
    rankA = dict(base, g_wlr=wlr, g_up=gup, g_bias=gb, c_mask=masks(le, gt), **shared,

                 cosT=np.ascontiguousarray(np.concatenate([cosG[:, :NRL], cosG[:, 2 * NRL:]], axis=1)),
                 sinT=np.ascontiguousarray(np.concatenate([sinG[:, :NRL], sinG[:, 2 * NRL:]], axis=1)),
                 selw=np.ascontiguousarray(np.tile(np.array([[0.0, 1.0]], np.float32), (128, 1))))
    rankB = dict(base, g_wlr=wlr_sw, g_up=np.ascontiguousarray(gup[:, ::-1]), g_bias=np.ascontiguousarray(gb[:, ::-1]),

                 c_mask=masks(lt, ge), **shared,
                 cosT=np.ascontiguousarray(np.concatenate([cosG[:, NRL:2 * NRL][:, ::-1], zc + 1.0], axis=1)),
                 sinT=np.ascontiguousarray(np.concatenate([sinG[:, NRL:2 * NRL][:, ::-1], zc], axis=1)),
                 selw=np.ascontiguousarray(np.tile(np.array([[1.0, 0.0]], np.float32), (128, 1))))
    maps = []
    for b in range(B):
        ma = dict(rankA)
        ma["h0"] = np.ascontiguousarray(np.concatenate([x[b, :NRL], meta], axis=0))
        mb = dict(rankB)
        mb["h0"] = np.ascontiguousarray(np.concatenate([x[b, NRL:][::-1], np.zeros((NMETA, D), np.float32)], axis=0))
        maps += [ma, mb]
    return maps


def assemble_pairs(results, B, NRL):
    out = np.empty((B, 2 * NRL, D), np.float32)
    for b in range(B):
        out[b, :NRL] = np.asarray(results[2 * b]["out"]).reshape(NRL, D)
        out[b, NRL:] = np.asarray(results[2 * b + 1]["out"]).reshape(NRL, D)[::-1]
    return out


def make_in_maps(inputs, NR, nb):
    f = lambda a: np.ascontiguousarray(np.asarray(a, dtype=np.float32))
    x = f(inputs["x"])
    meta = f(inputs["meta_tokens"])
    shared = dict(
        prenT=np.ascontiguousarray(f(inputs["pre_norm"]).reshape(4, KT, 128).transpose(0, 2, 1)),
        postn=f(inputs["post_norm"]),
        a_win=f(inputs["attn_w_in"]), a_wout=f(inputs["attn_w_out"]),
        a_qn=np.ascontiguousarray(f(inputs["attn_q_norm"]).T), a_kn=np.ascontiguousarray(f(inputs["attn_k_norm"]).T),
        g_win=f(inputs["gla_w_in"]), g_wout=f(inputs["gla_w_out"]),
        g_up=f(inputs["gla_gk_up"]), g_bias=f(inputs["gla_gk_bias"]), g_on=f(inputs["gla_o_norm"]),
    )
    shared.update(make_consts(NR))
    maps = []
    for b in range(nb):
        m = dict(shared)
        m["h0"] = np.ascontiguousarray(np.concatenate([x[b], meta], axis=0))
        maps.append(m)
    return maps


def kernel(**inputs):
    x = np.asarray(inputs["x"])
    B, SEQ, _ = x.shape
    NRL = SEQ // 2
    nc = Builder(NRL, groups=[[2 * b, 2 * b + 1] for b in range(B)]).build()
    maps = make_in_maps_pairs(inputs, NRL)
    res = run_bass_kernel_spmd(nc, maps, core_ids=list(range(2 * B)))
    return assemble_pairs(res.results, B, NRL)
```
